# Optimizing a Trainium2 kernel written in Bass

```python
import jax, jax.numpy as jnp
from jax import lax
import numpy as np

D_MODEL = 2048
BATCH = 4
SEQ = 2048
DEPTH = 4

GRID_W = 64
CTX_LEN = 256
N_MIXERS = 2
EXPAND = 2
D_BRANCH = EXPAND * D_MODEL
RET_HEADS = 8
RET_QK_DIM = D_MODEL // RET_HEADS
RET_V_DIM = D_BRANCH // RET_HEADS
RET_CHUNK = 128
NA_HEADS = 32
NA_HEAD_DIM = D_BRANCH // NA_HEADS
NA_WIN_H = 8
NA_WIN_W = 16
NA_QBLOCK_W = 16
ROPE_BASE = 10000.0
EPS = 1e-6

kernel_name = 'hybrid_retention_natten_prefix_dit'


def rmsnorm(x, g):
    xf = x.astype(jnp.float32)
    y = xf * lax.rsqrt(jnp.mean(xf * xf, axis=-1, keepdims=True) + EPS)
    return (y * g.astype(jnp.float32)).astype(x.dtype)


def head_norm(x):
    xf = x.astype(jnp.float32)
    return (xf * lax.rsqrt(jnp.mean(xf * xf, axis=-1, keepdims=True) + EPS)).astype(x.dtype)


def to_heads(a, n_heads):
    b, t, _ = a.shape
    return a.reshape(b, t, n_heads, -1).transpose(0, 2, 1, 3)


def from_heads(a):
    b, h, t, d = a.shape
    return a.transpose(0, 2, 1, 3).reshape(b, t, h * d)


def rope_1d(x, pos):
    half = x.shape[-1] // 2
    freqs = ROPE_BASE ** (-jnp.arange(half, dtype=jnp.float32) / half)
    ang = pos.astype(jnp.float32)[:, None] * freqs[None, :]
    cos = jnp.cos(ang).astype(x.dtype)
    sin = jnp.sin(ang).astype(x.dtype)
    x1, x2 = x[..., :half], x[..., half:]
    return jnp.concatenate([x1 * cos - x2 * sin, x1 * sin + x2 * cos], axis=-1)


def rope_2d(x, row, col):
    a = x.shape[-1] // 2
    return jnp.concatenate([rope_1d(x[..., :a], row), rope_1d(x[..., a:], col)], axis=-1)


def retention_scan(q, k, v, log_g, s0):
    b, h, t, dk = q.shape
    dv = v.shape[-1]
    n = t // RET_CHUNK
    pos = jnp.arange(RET_CHUNK, dtype=jnp.float32)
    diff = pos[:, None] - pos[None, :]
    lg = log_g[:, None, None]
    intra = jnp.where(diff >= 0, jnp.exp(lg * jnp.maximum(diff, 0.0)), 0.0).astype(q.dtype)
    q_dec = jnp.exp(log_g[:, None] * (pos + 1.0))[..., None].astype(q.dtype)
    k_dec = jnp.exp(log_g[:, None] * (RET_CHUNK - 1.0 - pos))[..., None].astype(q.dtype)
    c_dec = jnp.exp(log_g * RET_CHUNK)[:, None, None].astype(q.dtype)

    def chunks(a):
        return a.reshape(b, h, n, RET_CHUNK, a.shape[-1]).transpose(2, 0, 1, 3, 4)

    def step(s, inp):
        qc, kc, vc = inp
        scores = jnp.einsum('bhid,bhjd->bhij', qc, kc) * intra
        o = jnp.einsum('bhij,bhjv->bhiv', scores, vc) + jnp.einsum('bhid,bhdv->bhiv', qc * q_dec, s)
        s_new = s * c_dec + jnp.einsum('bhjd,bhjv->bhdv', kc * k_dec, vc)
        return s_new.astype(s.dtype), o

    s_fin, o = lax.scan(step, s0, (chunks(q), chunks(k), chunks(v)))
    return o.transpose(1, 2, 0, 3, 4).reshape(b, h, t, dv), s_fin


def retention_branch(h, hc, w_in, raw_fwd, raw_bwd, w_out, row_pos, col_pos, with_ctx_out):
    b = h.shape[0]
    cut = [D_MODEL, 2 * D_MODEL, 2 * D_MODEL + D_BRANCH]
    q, k, v, g = jnp.split(h @ w_in, cut, axis=-1)
    qc, kc, vc, gc = jnp.split(hc @ w_in, cut, axis=-1)
    q, k, v = to_heads(q, RET_HEADS), to_heads(k, RET_HEADS), to_heads(v, RET_HEADS)
    qc, kc, vc = to_heads(qc, RET_HEADS), to_heads(kc, RET_HEADS), to_heads(vc, RET_HEADS)
    q = rope_2d(q, row_pos, col_pos)
    k = rope_2d(k, row_pos, col_pos) * (RET_QK_DIM ** -0.5)
    kc = kc * (RET_QK_DIM ** -0.5)
    log_fwd = -jax.nn.softplus(raw_fwd.astype(jnp.float32))
    log_bwd = -jax.nn.softplus(raw_bwd.astype(jnp.float32))
    flip = lambda a: a[:, :, ::-1]
    zero = jnp.zeros((b, RET_HEADS, RET_QK_DIM, RET_V_DIM), h.dtype)
    oc_f, s_f = retention_scan(qc, kc, vc, log_fwd, zero)
    oc_b, s_b = retention_scan(flip(qc), flip(kc), flip(vc), log_bwd, zero)
    o_f, _ = retention_scan(q, k, v, log_fwd, s_f)
    o_b, _ = retention_scan(flip(q), flip(k), flip(v), log_bwd, s_b)
    o = head_norm(o_f + flip(o_b))
    y = (jax.nn.silu(g) * from_heads(o)) @ w_out
    if with_ctx_out:
        oc = head_norm(oc_f + flip(oc_b))
        yc = (jax.nn.silu(gc) * from_heads(oc)) @ w_out
        return y, yc
    return y, None


def na_branch(h, hc, w_in, rpb, w_out, with_ctx_out):
    b, t, _ = h.shape
    rows = t // GRID_W
    wh = min(NA_WIN_H, rows)
    scale = NA_HEAD_DIM ** -0.5
    q, k, v, g = jnp.split(h @ w_in, 4, axis=-1)
    qc, kc, vc, gc = jnp.split(hc @ w_in, 4, axis=-1)
    q, k, v = to_heads(q, NA_HEADS) * scale, to_heads(k, NA_HEADS), to_heads(v, NA_HEADS)
    qc, kc, vc = to_heads(qc, NA_HEADS) * scale, to_heads(kc, NA_HEADS), to_heads(vc, NA_HEADS)

    nqb = GRID_W // NA_QBLOCK_W
    kbw = NA_QBLOCK_W + NA_WIN_W
    qcol = np.arange(GRID_W).reshape(nqb, NA_QBLOCK_W)
    cb_start = np.clip(qcol[:, 0] - NA_WIN_W // 2, 0, GRID_W - kbw)
    kcol = cb_start[:, None] + np.arange(kbw)
    win_start = np.clip(qcol - NA_WIN_W // 2, 0, GRID_W - NA_WIN_W)
    col_valid = (kcol[:, None, :] >= win_start[..., None]) & (kcol[:, None, :] < win_start[..., None] + NA_WIN_W)
    col_idx = np.clip(kcol[:, None, :] - qcol[..., None] + NA_WIN_W - 1, 0, 2 * NA_WIN_W - 2)
    mask = jnp.asarray(np.broadcast_to(col_valid[:, :, None, :], (nqb, NA_QBLOCK_W, wh, kbw)).reshape(nqb, NA_QBLOCK_W, wh * kbw))

    q_grid = q.reshape(b, NA_HEADS, rows, GRID_W, NA_HEAD_DIM)
    k_grid = k.reshape(b, NA_HEADS, rows, GRID_W, NA_HEAD_DIM)
    v_grid = v.reshape(b, NA_HEADS, rows, GRID_W, NA_HEAD_DIM)

    def gather_block(a_grid, rs):
        a_rows = lax.dynamic_slice_in_dim(a_grid, rs, wh, axis=2)
        a_blk = a_rows[:, :, :, kcol]
        return a_blk.transpose(0, 1, 3, 2, 4, 5).reshape(b, NA_HEADS, nqb, wh * kbw, NA_HEAD_DIM)

    def row_fn(r):
        rs = jnp.clip(r - wh // 2, 0, rows - wh)
        k_blk = gather_block(k_grid, rs)
        v_blk = gather_block(v_grid, rs)
        q_r = lax.dynamic_index_in_dim(q_grid, r, axis=2, keepdims=False).reshape(b, NA_HEADS, nqb, NA_QBLOCK_W, NA_HEAD_DIM)
        row_idx = rs + jnp.arange(wh) - r + NA_WIN_H - 1
        bias = rpb[:, row_idx][:, :, col_idx]
        bias = bias.transpose(0, 2, 3, 1, 4).reshape(NA_HEADS, nqb, NA_QBLOCK_W, wh * kbw)
        s_loc = jnp.einsum('bhnqd,bhnkd->bhnqk', q_r, k_blk) + bias
        s_loc = jnp.where(mask, s_loc.astype(jnp.float32), -jnp.inf)
        s_ctx = jnp.einsum('bhnqd,bhcd->bhnqc', q_r, kc).astype(jnp.float32)
        p = jax.nn.softmax(jnp.concatenate([s_loc, s_ctx], axis=-1), axis=-1).astype(v.dtype)
        p_loc, p_ctx = p[..., :wh * kbw], p[..., wh * kbw:]
        o = jnp.einsum('bhnqk,bhnkd->bhnqd', p_loc, v_blk) + jnp.einsum('bhnqc,bhcd->bhnqd', p_ctx, vc)
        return o.reshape(b, NA_HEADS, GRID_W, NA_HEAD_DIM)

    o = lax.map(row_fn, jnp.arange(rows))
    o = o.transpose(1, 0, 3, 2, 4).reshape(b, t, NA_HEADS * NA_HEAD_DIM)
    y = (jax.nn.silu(g) * o) @ w_out
    if with_ctx_out:
        sc = jnp.einsum('bhqd,bhkd->bhqk', qc, kc).astype(jnp.float32)
        pc = jax.nn.softmax(sc, axis=-1).astype(vc.dtype)
        oc = from_heads(jnp.einsum('bhqk,bhkd->bhqd', pc, vc))
        yc = (jax.nn.silu(gc) * oc) @ w_out
        return y, yc
    return y, None


def setup_inputs(seed: int = 0) -> dict:
    key = jax.random.key(seed)
    ks = jax.random.split(key, 16)
    f32 = jnp.float32
    n_ret = (DEPTH + N_MIXERS - 1) // N_MIXERS
    n_na = DEPTH // N_MIXERS
    nrm = lambda k, s: jax.random.normal(k, s, f32)
    x = nrm(ks[0], (BATCH, SEQ, D_MODEL))
    c = nrm(ks[1], (BATCH, D_MODEL))
    ctx = nrm(ks[2], (BATCH, CTX_LEN, D_MODEL))
    c_ctx = nrm(ks[3], (D_MODEL,))
    mod_w = nrm(ks[4], (DEPTH, D_MODEL, 3 * D_MODEL)) * (0.5 * D_MODEL ** -0.5)
    mod_b = 0.01 * nrm(ks[5], (DEPTH, 3 * D_MODEL))
    norm_g = 1.0 + 0.02 * nrm(ks[6], (DEPTH, D_MODEL))
    ret_w_in = nrm(ks[7], (n_ret, D_MODEL, 2 * D_MODEL + 2 * D_BRANCH)) * (D_MODEL ** -0.5)
    neg_log_gamma = -jnp.log1p(-(2.0 ** (-5.0 - jnp.arange(RET_HEADS, dtype=f32))))
    raw = jnp.log(jnp.expm1(neg_log_gamma))
    ret_decay_fwd = raw[None] + 0.1 * nrm(ks[8], (n_ret, RET_HEADS))
    ret_decay_bwd = raw[None] + 0.1 * nrm(ks[9], (n_ret, RET_HEADS))
    ret_w_out = nrm(ks[10], (n_ret, D_BRANCH, D_MODEL)) * (D_BRANCH ** -0.5)
    na_w_in = nrm(ks[11], (n_na, D_MODEL, 4 * D_BRANCH)) * (D_MODEL ** -0.5)
    na_rpb = 0.02 * nrm(ks[12], (n_na, NA_HEADS, 2 * NA_WIN_H - 1, 2 * NA_WIN_W - 1))
    na_w_out = nrm(ks[13], (n_na, D_BRANCH, D_MODEL)) * (D_BRANCH ** -0.5)
    final_g = 1.0 + 0.02 * nrm(ks[14], (D_MODEL,))
    return {'x': x, 'c': c, 'ctx': ctx, 'c_ctx': c_ctx, 'mod_w': mod_w, 'mod_b': mod_b, 'norm_g': norm_g,
            'ret_w_in': ret_w_in, 'ret_decay_fwd': ret_decay_fwd, 'ret_decay_bwd': ret_decay_bwd, 'ret_w_out': ret_w_out,
            'na_w_in': na_w_in, 'na_rpb': na_rpb, 'na_w_out': na_w_out, 'final_g': final_g}


def reference(x, c, ctx, c_ctx, mod_w, mod_b, norm_g, ret_w_in, ret_decay_fwd, ret_decay_bwd, ret_w_out,
              na_w_in, na_rpb, na_w_out, final_g):
    n_tok = x.shape[1]
    t = jnp.arange(n_tok)
    row_pos = t // GRID_W
    col_pos = t % GRID_W
    cond_lat = jax.nn.silu(c)
    cond_ctx = jax.nn.silu(c_ctx)
    for l in range(DEPTH):
        last = l == DEPTH - 1
        sh, sc, gt = jnp.split(cond_lat @ mod_w[l] + mod_b[l], 3, axis=-1)
        shc, scc, gtc = jnp.split(cond_ctx @ mod_w[l] + mod_b[l], 3, axis=-1)
        h = rmsnorm(x, norm_g[l]) * (1.0 + sc[:, None]) + sh[:, None]
        hc = rmsnorm(ctx, norm_g[l]) * (1.0 + scc) + shc
        j = l // N_MIXERS
        if l % N_MIXERS == 0:
            y, yc = retention_branch(h, hc, ret_w_in[j], ret_decay_fwd[j], ret_decay_bwd[j], ret_w_out[j],
                                     row_pos, col_pos, not last)
        else:
            y, yc = na_branch(h, hc, na_w_in[j], na_rpb[j], na_w_out[j], not last)
        x = x + gt[:, None] * y
        if not last:
            ctx = ctx + gtc * yc
    return rmsnorm(x, final_g)
```

```python
import contextlib
import numpy as np
import ml_dtypes
import concourse.bass as bass
import concourse.mybir as mybir
from concourse.bass_utils import run_bass_kernel_spmd

F32 = mybir.dt.float32
BF16 = mybir.dt.bfloat16
ALU = mybir.AluOpType
AF = mybir.ActivationFunctionType

D = 2048
E = 4096
T = 2048
TC = 256
NKT = 16
EPS = 1e-6
MASKVAL = -30000.0
NA_SCALE = 128 ** -0.5
RET_KSCALE = 256 ** -0.5

ENGS = ["pe", "act", "dve", "pool", "sp"]
N_DMA_SEMS = 24


def g2p(g):
    if g == 0:
        return 0
    if g == 1:
        return 9
    c = g - 2
    return 1 + c if c < 8 else 10 + (c - 8)


class Buf:
    __slots__ = ("name", "w", "r")

    def __init__(self, name):
        self.name = name
        self.w = None
        self.r = []


class Sched:
    def __init__(self, nc, stack):
        self.nc = nc
        self.esem = {e: stack.enter_context(nc.semaphore(f"s_{e}")) for e in ENGS}
        self.ebase = {e: 0 for e in ENGS}
        self.dsem = {}
        self.dsem_val = {}
        self.dsem_next = {}
        for q in ["sp", "pool", "act"]:
            self.dsem[q] = [stack.enter_context(nc.semaphore(f"d_{q}{i}")) for i in range(N_DMA_SEMS)]
            self.dsem_val[q] = [0] * N_DMA_SEMS
            self.dsem_next[q] = 0
        self.epoch = 0
        self._reset()

    def _reset(self):
        self.ops = {e: [] for e in ENGS}
        self.waited = {e: {} for e in ENGS}
        self.dma_issued = []

    def buf(self, name="b"):
        return Buf(name)

    def bufs(self, n, name="b"):
        return [Buf(f"{name}{i}") for i in range(n)]

    def _live(self, t):
        return t is not None and t[-1] == self.epoch

    def _deps(self, eng, reads, writes):
        toks = []
        for b in reads:
            if self._live(b.w):
                toks.append(b.w)
        for b in writes:
            if self._live(b.w):
                if not (b.w[0] == "eng" and b.w[1] == eng):
                    toks.append(b.w)
            for t in b.r:
                if not self._live(t):
                    continue
                if t[0] == "eng" and t[1] == eng:
                    continue
                toks.append(t)
        out = []
        wd = self.waited[eng]
        for t in toks:
            if t[0] == "eng":
                key = ("eng", t[1]); val = t[2]
            else:
                key = ("dma", t[1], t[2]); val = t[3]
            if wd.get(key, -1) >= val:
                continue
            wd[key] = val
            out.append(t)
        return out

    def _mark(self, tok, reads, writes):
        for b in reads:
            if b.r and not self._live(b.r[0]):
                b.r = []
            b.r.append(tok)
        for b in writes:
            b.w = tok
            b.r = []

    def op(self, eng, fn, reads=(), writes=()):
        waits = self._deps(eng, reads, writes)
        idx = len(self.ops[eng])
        self.ops[eng].append(dict(kind="op", fn=fn, waits=waits, sig=False, idx=idx))
        tok = ("eng", eng, idx, self.epoch)
        self._mark(tok, reads, writes)
        return tok

    def dma(self, q, fn, reads=(), writes=()):
        waits = self._deps(q, reads, writes)
        k = self.dsem_next[q]
        self.dsem_next[q] = (k + 1) % N_DMA_SEMS
        prev = self.dsem_val[q][k]
        if prev > 0:
            key = ("dma", q, k)
            if self.waited[q].get(key, -1) < prev:
                self.waited[q][key] = prev
                waits.append(("dma", q, k, prev, self.epoch))
        val = prev + 16
        self.dsem_val[q][k] = val
        idx = len(self.ops[q])
        self.ops[q].append(dict(kind="dma", fn=fn, waits=waits, sig=False, idx=idx, dsem=(q, k)))
        tok = ("dma", q, k, val, self.epoch)
        self.dma_issued.append(tok)
        self._mark(tok, reads, writes)
        return tok

    def emit(self):
        nc = self.nc
        last = {}
        for t in self.dma_issued:
            last[(t[1], t[2])] = t
        for (q, k), t in last.items():
            if self.waited[q].get(("dma", q, k), -1) < t[3]:
                self.ops[q].append(dict(kind="wait", fn=None, waits=[t], sig=False, idx=len(self.ops[q])))
        for e in ENGS:
            for rec in self.ops[e]:
                for t in rec["waits"]:
                    if t[0] == "eng":
                        self.ops[t[1]][t[2]]["sig"] = True
        semval = {}
        for e in ENGS:
            c = self.ebase[e]
            for rec in self.ops[e]:
                if rec["kind"] == "op" and rec["sig"]:
                    c += 1
                    semval[(e, rec["idx"])] = c
            self.ebase[e] = c

        def make_body(e):
            ops = self.ops[e]

            def body(engine):
                for rec in ops:
                    for t in rec["waits"]:
                        if t[0] == "eng":
                            engine.wait_ge(self.esem[t[1]], semval[(t[1], t[2])])
                        else:
                            engine.wait_ge(self.dsem[t[1]][t[2]], t[3])
                    if rec["kind"] == "op":
                        ins = rec["fn"](engine)
                        if rec["sig"]:
                            ins.then_inc(self.esem[e], 1)
                    elif rec["kind"] == "dma":
                        ins = rec["fn"](engine)
                        q, k = rec["dsem"]
                        ins.then_inc(self.dsem[q][k], 16)
            return body

        with nc.Block() as block:
            block.tensor(make_body("pe"))
            block.scalar(make_body("act"))
            block.vector(make_body("dve"))
            block.gpsimd(make_body("pool"))
            block.sync(make_body("sp"))
        self.epoch += 1
        self._reset()


C_IDENT = 0
C_DPOS = 128
C_DNEG = 256
C_MLO = 384
C_MUP = 512
C_I1 = 640
C_I2 = 768
C_P1 = 896
C_P2 = 897
C_128 = 898
CW = 900


def make_const():
    c = np.zeros((128, CW), np.float32)
    j = np.arange(128)[:, None].astype(np.float32)
    i = np.arange(128)[None, :].astype(np.float32)
    c[:, C_IDENT:C_IDENT + 128] = np.eye(128, dtype=np.float32)
    c[:, C_DPOS:C_DPOS + 128] = np.maximum(i - j, 0)
    c[:, C_DNEG:C_DNEG + 128] = np.maximum(j - i, 0)
    c[:, C_MLO:C_MLO + 128] = (i >= j)
    c[:, C_MUP:C_MUP + 128] = (j >= i)
    c[:, C_I1:C_I1 + 128] = i + 1
    c[:, C_I2:C_I2 + 128] = 128 - i
    c[:, C_P1] = 127 - np.arange(128)
    c[:, C_P2] = np.arange(128)
    c[:, C_128] = 128.0
    return c


def make_rope():
    half = 64
    freqs = (np.float32(10000.0) ** (-np.arange(half, dtype=np.float32) / np.float32(half))).astype(np.float32)
    t = np.arange(T)
    row = (t // 64).astype(np.float32)
    col = (t % 64).astype(np.float32)
    ang_r = (row[:, None] * freqs[None, :]).astype(np.float32)
    ang_c = (col[:, None] * freqs[None, :]).astype(np.float32)
    cos = np.stack([np.cos(ang_r), np.cos(ang_c)], axis=1).astype(np.float32)
    sin = np.stack([np.sin(ang_r), np.sin(ang_c)], axis=1).astype(np.float32)
    cos = cos.reshape(16, 128, 2, 64).transpose(1, 0, 2, 3)
    sin = sin.reshape(16, 128, 2, 64).transpose(1, 0, 2, 3)
    return np.ascontiguousarray(cos), np.ascontiguousarray(sin)


def make_utab(rpb_h):
    qc = np.arange(64)[None, :]
    kc = np.arange(64)[:, None]
    ws = np.clip(qc - 8, 0, 48)
    valid = (kc >= ws) & (kc < ws + 16)
    dc = np.clip(kc - qc + 15, 0, 30)
    nh = rpb_h.shape[0]
    U = np.empty((nh, 128, 14, 64), np.float32)
    for jj in range(2):
        for e in range(14):
            dr = e + jj
            vals = rpb_h[:, dr][:, dc]
            U[:, jj * 64:(jj + 1) * 64, e, :] = np.where(valid[None], vals, np.float32(MASKVAL))
    return U


class Prog:
    def __init__(self, mode):
        self.mode = mode
        self.nc = bass.Bass("TRN2", target_bir_lowering=False)
        self.stack = contextlib.ExitStack()
        self.S = Sched(self.nc, self.stack)

    def din(self, name, shape, dt=F32):
        return self.nc.dram_tensor(name, list(shape), dt, kind="ExternalInput").ap()

    def dout(self, name, shape, dt=F32):
        return self.nc.dram_tensor(name, list(shape), dt, kind="ExternalOutput").ap()

    def dscr(self, name, shape, dt):
        return self.nc.dram_tensor(name, list(shape), dt).ap()

    def sb(self, st, name, shape, dt):
        self._n = getattr(self, "_n", 0) + 1
        return st.enter_context(self.nc.sbuf_tensor(f"sb{self._n}_{name}", list(shape), dt))

    def ps(self, st, name, shape, dt):
        self._n = getattr(self, "_n", 0) + 1
        return st.enter_context(self.nc.psum_tensor(f"ps{self._n}_{name}", list(shape), dt))

    def setup_persist(self, cvec_d, const_d):
        S, st = self.S, self.stack
        self.const = self.sb(st, "const", [128, CW], F32)
        self.b_const = S.buf("const")
        self.ident = self.sb(st, "ident", [128, 128], BF16)
        self.b_ident = S.buf("ident")
        self.condB = self.sb(st, "condB", [128, 2, NKT, 128], BF16)
        self.b_condB = S.buf("condB")
        with contextlib.ExitStack() as ph:
            cv = self.sb(ph, "cv", [128, 2, NKT], F32)
            cs = self.sb(ph, "cs", [128, 2, NKT], F32)
            b_cv, b_cs = S.bufs(2)
            S.dma("sp", lambda e: e.dma_start(out=self.const[:], in_=const_d), writes=[self.b_const])
            S.dma("sp", lambda e: e.dma_start(out=cv[:], in_=cvec_d), writes=[b_cv])
            S.op("dve", lambda e: e.tensor_copy(out=self.ident[:], in_=self.const[:, C_IDENT:C_IDENT + 128]),
                 reads=[self.b_const], writes=[self.b_ident])
            S.op("act", lambda e: e.activation(out=cs[:], in_=cv[:], func=AF.Silu), reads=[b_cv], writes=[b_cs])
            for c in range(2):
                S.op("dve", lambda e, c=c: e.tensor_copy(
                    out=self.condB[:, c], in_=cs[:, c, :].unsqueeze(2).to_broadcast([128, NKT, 128])),
                    reads=[b_cs], writes=[self.b_condB])
            S.emit()

    def alloc_M(self, st):
        self.M = [self.sb(st, f"M{i}", [128, D], F32) for i in range(4)]
        self.b_M = self.S.bufs(4, "M")

    def phase_mod(self, modw_d, modb_d, col0, ncols, dests):
        S = self.S
        nchunk = ncols // 512
        with contextlib.ExitStack() as ph:
            wch = [self.sb(ph, f"mw{i}", [128, NKT, 512], BF16) for i in range(2)]
            b_wch = S.bufs(2, "mw")
            mb = [self.sb(ph, f"mb{i}", [128, 512], F32) for i in range(2)]
            b_mb = S.bufs(2, "mb")
            pp = [self.ps(ph, f"pm{i}", [128, 512], F32) for i in range(4)]
            b_pp = S.bufs(4, "pm")
            wv = modw_d.rearrange("(kt p) n -> p kt n", p=128)
            n = 0
            for ch in range(nchunk):
                cc = col0 + ch * 512
                w, bw = wch[ch % 2], b_wch[ch % 2]
                S.dma("pool", lambda e, w=w, cc=cc: e.dma_start(out=w[:], in_=wv[:, :, cc:cc + 512]), writes=[bw])
                m, bm = mb[ch % 2], b_mb[ch % 2]
                S.dma("sp", lambda e, m=m, cc=cc: e.dma_start(out=m[:], in_=modb_d[:, cc:cc + 512].partition_broadcast(128)),
                      writes=[bm])
                blk = (ch * 512) // D
                off = (ch * 512) % D
                for c in range(2):
                    if (c, blk) not in dests:
                        continue
                    p, bp = pp[n % 4], b_pp[n % 4]
                    n += 1
                    for kt in range(NKT):
                        S.op("pe", lambda e, p=p, c=c, kt=kt, w=w: e.matmul(
                            p[:], lhsT=self.condB[:, c, kt, :], rhs=w[:, kt, :], start=(kt == 0), stop=(kt == NKT - 1)),
                            reads=[self.b_condB, bw], writes=[bp])
                    mi = dests[(c, blk)]
                    S.op("dve", lambda e, p=p, m=m, mi=mi, off=off: e.tensor_tensor(
                        out=self.M[mi][:, off:off + 512], in0=p[:], in1=m[:], op=ALU.add),
                        reads=[bp, bm], writes=[self.b_M[mi]])
            S.emit()

    def phase_gs(self, ng_d, pairs):
        S = self.S
        with contextlib.ExitStack() as ph:
            ng = self.sb(ph, "ng", [128, D], F32)
            b_ng = S.buf()
            S.dma("sp", lambda e: e.dma_start(out=ng[:], in_=ng_d.partition_broadcast(128)), writes=[b_ng])
            for mi in pairs:
                S.op("dve", lambda e, mi=mi: e.scalar_tensor_tensor(
                    out=self.M[mi][:], in0=self.M[mi][:], scalar=1.0, in1=ng[:], op0=ALU.add, op1=ALU.mult),
                    reads=[self.b_M[mi], b_ng], writes=[self.b_M[mi]])
            S.emit()

    def phase_norm(self, x_src, positions, hT_d, mod_of_pos):
        S = self.S
        with contextlib.ExitStack() as ph:
            xt = [self.sb(ph, f"xt{i}", [128, D], F32) for i in range(2)]
            b_xt = S.bufs(2, "xt")
            junk = self.sb(ph, "junk", [128, D], BF16)
            b_junk = S.buf()
            ss = [self.sb(ph, f"ss{i}", [128, 1], F32) for i in range(2)]
            b_ss = S.bufs(2, "ss")
            t1 = [self.sb(ph, f"t1{i}", [128, D], F32) for i in range(2)]
            b_t1 = S.bufs(2, "t1")
            hb = [self.sb(ph, f"hb{i}", [128, D], BF16) for i in range(2)]
            b_hb = S.bufs(2, "hb")
            hs = [self.sb(ph, f"hs{i}", [128, NKT, 128], BF16) for i in range(2)]
            b_hs = S.bufs(2, "hs")
            pt = [self.ps(ph, f"pt{i}", [128, 8, 128], BF16) for i in range(4)]
            b_pt = S.bufs(4, "pt")
            for n, pos in enumerate(positions):
                i = n % 2
                shi, gsi = mod_of_pos(pos)
                S.dma("sp", lambda e, i=i, pos=pos: e.dma_start(out=xt[i][:], in_=x_src(pos)), writes=[b_xt[i]])
                S.op("act", lambda e, i=i: e.activation(out=junk[:], in_=xt[i][:], func=AF.Square, scale=float(D) ** -0.5, accum_out=ss[i][:]),
                     reads=[b_xt[i]], writes=[b_junk, b_ss[i]])
                S.op("dve", lambda e, i=i: e.tensor_scalar(out=ss[i][:], in0=ss[i][:], scalar1=EPS, scalar2=None, op0=ALU.add),
                     reads=[b_ss[i]], writes=[b_ss[i]])
                S.op("act", lambda e, i=i: e.sqrt(out=ss[i][:], in_=ss[i][:]), reads=[b_ss[i]], writes=[b_ss[i]])
                S.op("dve", lambda e, i=i: e.reciprocal(out=ss[i][:], in_=ss[i][:]), reads=[b_ss[i]], writes=[b_ss[i]])
                S.op("dve", lambda e, i=i, gsi=gsi: e.scalar_tensor_tensor(
                    out=t1[i][:], in0=xt[i][:], scalar=ss[i][:, 0:1], in1=self.M[gsi][:], op0=ALU.mult, op1=ALU.mult),
                    reads=[b_xt[i], b_ss[i], self.b_M[gsi]], writes=[b_t1[i]])
                S.op("pool", lambda e, i=i, shi=shi: e.tensor_tensor(out=hb[i][:], in0=t1[i][:], in1=self.M[shi][:], op=ALU.add),
                     reads=[b_t1[i], self.b_M[shi]], writes=[b_hb[i]])
                for half in range(2):
                    pi = (2 * n + half) % 4
                    for k8 in range(8):
                        kt = half * 8 + k8
                        S.op("pe", lambda e, i=i, pi=pi, k8=k8, kt=kt: e.transpose(
                            out=pt[pi][:, k8, :], in_=hb[i][:, kt * 128:(kt + 1) * 128], identity=self.ident[:]),
                            reads=[b_hb[i], self.b_ident], writes=[b_pt[pi]])
                    eng = "act" if half == 0 else "dve"
                    if eng == "act":
                        S.op("act", lambda e, i=i, pi=pi, half=half: e.copy(out=hs[i][:, half * 8:(half + 1) * 8, :], in_=pt[pi][:]),
                             reads=[b_pt[pi]], writes=[b_hs[i]])
                    else:
                        S.op("dve", lambda e, i=i, pi=pi, half=half: e.tensor_copy(out=hs[i][:, half * 8:(half + 1) * 8, :], in_=pt[pi][:]),
                             reads=[b_pt[pi]], writes=[b_hs[i]])
                S.dma("sp", lambda e, i=i, pos=pos: e.dma_start(out=hT_d[pos], in_=hs[i][:]), reads=[b_hs[i]], writes=[self.b_hT])
            S.emit()

    def phase_ret(self, hT_d, win_d, dec_d, cos_d, sin_d, AT_d):
        S = self.S
        order_b = [1, 0] + list(range(17, 1, -1))
        wv = win_d.rearrange("(kt p) n -> p kt n", p=128)
        with contextlib.ExitStack() as ph:
            wch = [self.sb(ph, f"rw{i}", [128, NKT, 512], BF16) for i in range(2)]
            b_wch = S.bufs(2, "rw")
            hts = [self.sb(ph, f"hts{i}", [128, NKT, 128], BF16) for i in range(3)]
            b_hts = S.bufs(3, "hts")
            QK = self.sb(ph, "QK", [128, 18, 512], BF16)
            V = self.sb(ph, "V", [128, 18, 512], BF16)
            G = self.sb(ph, "G", [128, 18, 512], BF16)
            b_QK = S.bufs(18, "QK"); b_V = S.bufs(18, "V"); b_G = S.bufs(18, "G")
            SBS = self.sb(ph, "SBS", [128, 18, 2, 512], BF16)
            b_SBS = S.bufs(18, "SBS")
            COS = self.sb(ph, "COS", [128, 16, 2, 64], F32)
            SIN = self.sb(ph, "SIN", [128, 16, 2, 64], F32)
            b_rope = S.buf("rope")
            raw = [self.sb(ph, f"raw{i}", [128, 512], F32) for i in range(2)]
            b_raw = S.bufs(2, "raw")
            rt = [self.sb(ph, f"rt{i}", [128, 2, 2, 64], F32) for i in range(4)]
            b_rt = S.bufs(4, "rt")
            Sst = [self.sb(ph, f"Sst{i}", [128, 2, 512], F32) for i in range(2)]
            b_Sst = S.bufs(2, "Sst")
            sfb = [self.sb(ph, f"sfb{i}", [128, 2, 512], BF16) for i in range(2)]
            b_sfb = S.bufs(2, "sfb")
            ksc = [self.sb(ph, f"ksc{i}", [128, 256], BF16) for i in range(2)]
            b_ksc = S.bufs(2, "ksc")
            qkT = [self.sb(ph, f"qkT{i}", [128, 4, 128], BF16) for i in range(2)]
            b_qkT = S.bufs(2, "qkT")
            qfT = [self.sb(ph, f"qfT{i}", [128, 2, 128], BF16) for i in range(2)]
            b_qfT = S.bufs(2, "qfT")
            qbT = [self.sb(ph, f"qbT{i}", [128, 2, 128], BF16) for i in range(2)]
            b_qbT = S.bufs(2, "qbT")
            Pm = [self.sb(ph, f"Pm{i}", [128, 128], BF16) for i in range(2)]
            b_Pm = S.bufs(2, "Pm")
            junk = self.sb(ph, "rjunk", [128, 512], BF16)
            b_junk = S.buf()
            hss = [self.sb(ph, f"hss{i}", [128, 1], F32) for i in range(2)]
            b_hss = S.bufs(2, "hss")
            Atok = [self.sb(ph, f"Atok{i}", [128, 512], BF16) for i in range(2)]
            b_Atok = S.bufs(2, "Atok")
            ATst = [self.sb(ph, f"ATst{i}", [128, 4, 128], BF16) for i in range(2)]
            b_ATst = S.bufs(2, "ATst")
            dec = self.sb(ph, "dec", [128, 8], F32)
            lg = self.sb(ph, "lg", [128, 8], F32)
            b_dec, b_lg = S.bufs(2)
            Mt = self.sb(ph, "Mt", [128, 128], F32)
            Mt2 = self.sb(ph, "Mt2", [128, 128], F32)
            QDF = self.sb(ph, "QDF", [128, 128], BF16)
            QDB = self.sb(ph, "QDB", [128, 128], BF16)
            KD = self.sb(ph, "KD", [128, 4], F32)
            b_tab = S.buf("tab")
            pin = [self.ps(ph, f"pin{i}", [128, 512], F32) for i in range(2)]
            b_pin = S.bufs(2, "pin")
            pS = [self.ps(ph, f"pS{i}", [128, 512], F32) for i in range(2)]
            b_pS = S.bufs(2, "pS")
            po = [self.ps(ph, f"po{i}", [128, 512], F32) for i in range(2)]
            b_po = S.bufs(2, "po")
            psc = self.ps(ph, "psc", [128, 128], F32)
            b_psc = S.buf()
            ptr = self.ps(ph, "ptr", [128, 8, 128], BF16)
            b_ptrq, b_ptra = S.bufs(2)

            S.dma("sp", lambda e: e.dma_start(out=COS[:], in_=cos_d), writes=[b_rope])
            S.dma("sp", lambda e: e.dma_start(out=SIN[:], in_=sin_d), writes=[b_rope])
            S.dma("sp", lambda e: e.dma_start(out=dec[:], in_=dec_d.partition_broadcast(128)), writes=[b_dec])
            S.op("act", lambda e: e.activation(out=lg[:], in_=dec[:], func=AF.Exp), reads=[b_dec], writes=[b_lg])
            S.op("dve", lambda e: e.tensor_scalar(out=lg[:], in0=lg[:], scalar1=1.0, scalar2=None, op0=ALU.add),
                 reads=[b_lg], writes=[b_lg])
            S.op("act", lambda e: e.activation(out=lg[:], in_=lg[:], func=AF.Ln), reads=[b_lg], writes=[b_lg])
            S.op("dve", lambda e: e.tensor_scalar(out=lg[:], in0=lg[:], scalar1=-1.0, scalar2=None, op0=ALU.mult),
                 reads=[b_lg], writes=[b_lg])
            C = self.const
            nin = 0
            nh = 0
            for hh in range(4):
                lgf = lg[:, hh:hh + 1]
                lgb = lg[:, 4 + hh:5 + hh]
                S.op("act", lambda e, lgf=lgf: e.activation(out=Mt[:], in_=C[:, C_DPOS:C_DPOS + 128], func=AF.Exp, scale=lgf),
                     reads=[b_lg, self.b_const], writes=[b_tab])
                S.op("act", lambda e, lgb=lgb: e.activation(out=Mt2[:], in_=C[:, C_DNEG:C_DNEG + 128], func=AF.Exp, scale=lgb),
                     reads=[b_lg, self.b_const], writes=[b_tab])
                S.op("dve", lambda e: e.tensor_tensor(out=Mt[:], in0=Mt[:], in1=C[:, C_MLO:C_MLO + 128], op=ALU.mult),
                     reads=[b_tab], writes=[b_tab])
                S.op("dve", lambda e: e.tensor_tensor(out=Mt2[:], in0=Mt2[:], in1=C[:, C_MUP:C_MUP + 128], op=ALU.mult),
                     reads=[b_tab], writes=[b_tab])
                S.op("dve", lambda e: e.tensor_tensor(out=Mt[:], in0=Mt[:], in1=Mt2[:], op=ALU.add), reads=[b_tab], writes=[b_tab])
                S.op("dve", lambda e: e.tensor_scalar(out=Mt[:], in0=Mt[:], scalar1=RET_KSCALE, scalar2=None, op0=ALU.mult),
                     reads=[b_tab], writes=[b_tab])
                S.op("act", lambda e, lgf=lgf: e.activation(out=QDF[:], in_=C[:, C_I1:C_I1 + 128], func=AF.Exp, scale=lgf),
                     reads=[b_lg, self.b_const], writes=[b_tab])
                S.op("act", lambda e, lgb=lgb: e.activation(out=QDB[:], in_=C[:, C_I2:C_I2 + 128], func=AF.Exp, scale=lgb),
                     reads=[b_lg, self.b_const], writes=[b_tab])
                S.op("act", lambda e, lgf=lgf: e.activation(out=KD[:, 0:1], in_=C[:, C_P1:C_P1 + 1], func=AF.Exp, scale=lgf),
                     reads=[b_lg, self.b_const], writes=[b_tab])
                S.op("act", lambda e, lgb=lgb: e.activation(out=KD[:, 1:2], in_=C[:, C_P2:C_P2 + 1], func=AF.Exp, scale=lgb),
                     reads=[b_lg, self.b_const], writes=[b_tab])
                S.op("act", lambda e, lgf=lgf: e.activation(out=KD[:, 2:3], in_=C[:, C_128:C_128 + 1], func=AF.Exp, scale=lgf),
                     reads=[b_lg, self.b_const], writes=[b_tab])
                S.op("act", lambda e, lgb=lgb: e.activation(out=KD[:, 3:4], in_=C[:, C_128:C_128 + 1], func=AF.Exp, scale=lgb),
                     reads=[b_lg, self.b_const], writes=[b_tab])
                S.op("dve", lambda e: e.tensor_scalar(out=KD[:, 0:2], in0=KD[:, 0:2], scalar1=RET_KSCALE, scalar2=None, op0=ALU.mult),
                     reads=[b_tab], writes=[b_tab])
                for ci in range(3):
                    w, bw = wch[nh % 2], b_wch[nh % 2]
                    nh += 1
                    cc = hh * 1536 + ci * 512
                    S.dma("pool", lambda e, w=w, cc=cc: e.dma_start(out=w[:], in_=wv[:, :, cc:cc + 512]), writes=[bw])
                    for g in range(18):
                        ht, bh = hts[nin % 3], b_hts[nin % 3]
                        p, bp = pin[nin % 2], b_pin[nin % 2]
                        nin += 1
                        pos = g2p(g)
                        S.dma("sp", lambda e, ht=ht, pos=pos: e.dma_start(out=ht[:], in_=hT_d[pos]), reads=[self.b_hT], writes=[bh])
                        for kt in range(NKT):
                            S.op("pe", lambda e, p=p, ht=ht, w=w, kt=kt: e.matmul(
                                p[:], lhsT=ht[:, kt, :], rhs=w[:, kt, :], start=(kt == 0), stop=(kt == NKT - 1)),
                                reads=[bh, bw], writes=[bp])
                        if ci == 0:
                            if g < 2:
                                S.op("act", lambda e, p=p, g=g: e.copy(out=QK[:, g, :], in_=p[:]), reads=[bp], writes=[b_QK[g]])
                            else:
                                r, br = raw[g % 2], b_raw[g % 2]
                                S.op("act", lambda e, p=p, r=r: e.copy(out=r[:], in_=p[:]), reads=[bp], writes=[br])
                                X = r[:].rearrange("p (a b c d) -> p a b c d", a=2, b=2, c=2)
                                O = QK[:, g, :].rearrange("p (a b c d) -> p a b c d", a=2, b=2, c=2)
                                A_, B_ = X[:, :, :, 0, :], X[:, :, :, 1, :]
                                cosb = COS[:, g - 2].unsqueeze(1).to_broadcast([128, 2, 2, 64])
                                sinb = SIN[:, g - 2].unsqueeze(1).to_broadcast([128, 2, 2, 64])
                                S.op("dve", lambda e, A_=A_, cosb=cosb: e.tensor_tensor(out=rt[0][:], in0=A_, in1=cosb, op=ALU.mult),
                                     reads=[br, b_rope], writes=[b_rt[0]])
                                S.op("pool", lambda e, B_=B_, sinb=sinb: e.tensor_tensor(out=rt[1][:], in0=B_, in1=sinb, op=ALU.mult),
                                     reads=[br, b_rope], writes=[b_rt[1]])
                                S.op("dve", lambda e, O=O: e.tensor_tensor(out=O[:, :, :, 0, :], in0=rt[0][:], in1=rt[1][:], op=ALU.subtract),
                                     reads=[b_rt[0], b_rt[1]], writes=[b_QK[g]])
                                S.op("pool", lambda e, A_=A_, sinb=sinb: e.tensor_tensor(out=rt[2][:], in0=A_, in1=sinb, op=ALU.mult),
                                     reads=[br, b_rope], writes=[b_rt[2]])
                                S.op("dve", lambda e, B_=B_, cosb=cosb: e.tensor_tensor(out=rt[3][:], in0=B_, in1=cosb, op=ALU.mult),
                                     reads=[br, b_rope], writes=[b_rt[3]])
                                S.op("pool", lambda e, O=O: e.tensor_tensor(out=O[:, :, :, 1, :], in0=rt[2][:], in1=rt[3][:], op=ALU.add),
                                     reads=[b_rt[2], b_rt[3]], writes=[b_QK[g]])
                        elif ci == 1:
                            S.op("act", lambda e, p=p, g=g: e.copy(out=V[:, g, :], in_=p[:]), reads=[bp], writes=[b_V[g]])
                        else:
                            S.op("act", lambda e, p=p, g=g: e.activation(out=G[:, g, :], in_=p[:], func=AF.Silu),
                                 reads=[bp], writes=[b_G[g]])
                Sb, bSb = Sst[1], b_Sst[1]
                S.op("pool", lambda e: e.memset(Sb[:], 0.0), writes=[bSb])
                for n, c in enumerate(order_b):
                    S.op("act", lambda e, c=c: e.copy(out=SBS[:, c], in_=Sb[:]), reads=[bSb], writes=[b_SBS[c]])
                    if n == len(order_b) - 1:
                        break
                    ks, bks = ksc[n % 2], b_ksc[n % 2]
                    S.op("pool", lambda e, ks=ks, c=c: e.tensor_scalar(out=ks[:], in0=QK[:, c, 256:512], scalar1=KD[:, 1:2], scalar2=None,
                                                                        op0=ALU.mult), reads=[b_QK[c], b_tab], writes=[bks])
                    for dh in range(2):
                        S.op("pe", lambda e, ks=ks, c=c, dh=dh: e.matmul(pS[dh][:], lhsT=ks[:, dh * 128:(dh + 1) * 128], rhs=V[:, c, :],
                                                                          start=True, stop=True), reads=[bks, b_V[c]], writes=[b_pS[dh]])
                        S.op("dve", lambda e, dh=dh: e.scalar_tensor_tensor(out=Sb[:, dh, :], in0=Sb[:, dh, :], scalar=KD[:, 3:4], in1=pS[dh][:],
                                                                             op0=ALU.mult, op1=ALU.add), reads=[bSb, b_pS[dh], b_tab], writes=[bSb])
                Sf, bSf = Sst[0], b_Sst[0]
                S.op("pool", lambda e: e.memset(Sf[:], 0.0), writes=[bSf])
                for c in range(18):
                    i2 = c % 2
                    S.op("act", lambda e, i2=i2: e.copy(out=sfb[i2][:], in_=Sf[:]), reads=[bSf], writes=[b_sfb[i2]])
                    for t4 in range(4):
                        S.op("pe", lambda e, c=c, t4=t4: e.transpose(out=ptr[:, t4, :], in_=QK[:, c, t4 * 128:(t4 + 1) * 128], identity=self.ident[:]),
                             reads=[b_QK[c], self.b_ident], writes=[b_ptrq])
                    S.op("dve", lambda e, i2=i2: e.tensor_copy(out=qkT[i2][:], in_=ptr[:, 0:4, :]), reads=[b_ptrq], writes=[b_qkT[i2]])
                    S.op("dve", lambda e, i2=i2: e.tensor_tensor(out=qfT[i2][:], in0=qkT[i2][:, 0:2, :],
                                                                 in1=QDF[:].unsqueeze(1).to_broadcast([128, 2, 128]), op=ALU.mult),
                         reads=[b_qkT[i2], b_tab], writes=[b_qfT[i2]])
                    S.op("pool", lambda e, i2=i2: e.tensor_tensor(out=qbT[i2][:], in0=qkT[i2][:, 0:2, :],
                                                                  in1=QDB[:].unsqueeze(1).to_broadcast([128, 2, 128]), op=ALU.mult),
                         reads=[b_qkT[i2], b_tab], writes=[b_qbT[i2]])
                    for dh in range(2):
                        S.op("pe", lambda e, i2=i2, dh=dh: e.matmul(psc[:], lhsT=qkT[i2][:, 2 + dh, :], rhs=qkT[i2][:, dh, :],
                                                                    start=(dh == 0), stop=(dh == 1)), reads=[b_qkT[i2]], writes=[b_psc])
                    S.op("dve", lambda e, i2=i2: e.tensor_tensor(out=Pm[i2][:], in0=psc[:], in1=Mt[:], op=ALU.mult),
                         reads=[b_psc, b_tab], writes=[b_Pm[i2]])
                    o, bo = po[i2], b_po[i2]
                    S.op("pe", lambda e, i2=i2, o=o, c=c: e.matmul(o[:], lhsT=Pm[i2][:], rhs=V[:, c, :], start=True, stop=False),
                         reads=[b_Pm[i2], b_V[c]], writes=[bo])
                    for dh in range(2):
                        S.op("pe", lambda e, i2=i2, o=o, dh=dh: e.matmul(o[:], lhsT=qfT[i2][:, dh, :], rhs=sfb[i2][:, dh, :], start=False, stop=False),
                             reads=[b_qfT[i2], b_sfb[i2]], writes=[bo])
                    for dh in range(2):
                        S.op("pe", lambda e, i2=i2, o=o, dh=dh, c=c: e.matmul(o[:], lhsT=qbT[i2][:, dh, :], rhs=SBS[:, c, dh, :], start=False, stop=(dh == 1)),
                             reads=[b_qbT[i2], b_SBS[c]], writes=[bo])
                    S.op("act", lambda e, i2=i2, o=o: e.activation(out=junk[:], in_=o[:], func=AF.Square, scale=512.0 ** -0.5, accum_out=hss[i2][:]),
                         reads=[bo], writes=[b_junk, b_hss[i2]])
                    S.op("dve", lambda e, i2=i2: e.tensor_scalar(out=hss[i2][:], in0=hss[i2][:], scalar1=EPS, scalar2=None, op0=ALU.add),
                         reads=[b_hss[i2]], writes=[b_hss[i2]])
                    S.op("act", lambda e, i2=i2: e.sqrt(out=hss[i2][:], in_=hss[i2][:]), reads=[b_hss[i2]], writes=[b_hss[i2]])
                    S.op("dve", lambda e, i2=i2: e.reciprocal(out=hss[i2][:], in_=hss[i2][:]), reads=[b_hss[i2]], writes=[b_hss[i2]])
                    S.op("dve", lambda e, i2=i2, o=o, c=c: e.scalar_tensor_tensor(out=Atok[i2][:], in0=o[:], scalar=hss[i2][:, 0:1], in1=G[:, c, :],
                                                                                   op0=ALU.mult, op1=ALU.mult),
                         reads=[bo, b_hss[i2], b_G[c]], writes=[b_Atok[i2]])
                    for e4 in range(4):
                        S.op("pe", lambda e, i2=i2, e4=e4: e.transpose(out=ptr[:, 4 + e4, :], in_=Atok[i2][:, e4 * 128:(e4 + 1) * 128], identity=self.ident[:]),
                             reads=[b_Atok[i2], self.b_ident], writes=[b_ptra])
                    S.op("act", lambda e, i2=i2: e.copy(out=ATst[i2][:], in_=ptr[:, 4:8, :]), reads=[b_ptra], writes=[b_ATst[i2]])
                    pos = g2p(c)
                    r_, tl = pos // 9, pos % 9
                    S.dma("sp", lambda e, i2=i2, r_=r_, tl=tl, hh=hh: e.dma_start(
                        out=AT_d[r_, hh * 4:(hh + 1) * 4, :, tl * 128:(tl + 1) * 128].rearrange("e p i -> p e i"), in_=ATst[i2][:]),
                        reads=[b_ATst[i2]], writes=[self.b_AT])
                    if c < 17:
                        ks, bks = ksc[c % 2], b_ksc[c % 2]
                        S.op("pool", lambda e, ks=ks, c=c: e.tensor_scalar(out=ks[:], in0=QK[:, c, 256:512], scalar1=KD[:, 0:1], scalar2=None,
                                                                            op0=ALU.mult), reads=[b_QK[c], b_tab], writes=[bks])
                        for dh in range(2):
                            S.op("pe", lambda e, ks=ks, c=c, dh=dh: e.matmul(pS[dh][:], lhsT=ks[:, dh * 128:(dh + 1) * 128], rhs=V[:, c, :],
                                                                              start=True, stop=True), reads=[bks, b_V[c]], writes=[b_pS[dh]])
                            S.op("dve", lambda e, dh=dh: e.scalar_tensor_tensor(out=Sf[:, dh, :], in0=Sf[:, dh, :], scalar=KD[:, 2:3], in1=pS[dh][:],
                                                                                 op0=ALU.mult, op1=ALU.add), reads=[bSf, b_pS[dh], b_tab], writes=[bSf])
            S.emit()

    def phase_na(self, hT_d, win_d, utab_d, AT_d):
        S = self.S
        wv = win_d.rearrange("(kt p) n -> p kt n", p=128)
        groups = [(0, 256, "ctx")] + [(256 + 512 * i, 512, 1 + 4 * i if i < 2 else 10 + 4 * (i - 2)) for i in range(4)]
        with contextlib.ExitStack() as ph:
            HT = self.sb(ph, "HT", [128, 18, NKT, 128], BF16)
            b_HT = S.bufs(18, "HT")
            wch = [self.sb(ph, f"nw{i}", [128, NKT, 512], BF16) for i in range(2)]
            b_wch = S.bufs(2, "nw")
            U = [self.sb(ph, f"U{i}", [128, 14, 64], F32) for i in range(2)]
            b_U = S.bufs(2, "U")
            QT = self.sb(ph, "QT", [128, 2304], BF16); b_QT = S.buf("QT")
            KT = self.sb(ph, "KT", [128, 2304], BF16); b_KT = S.buf("KT")
            GT = self.sb(ph, "GTn", [128, 2304], BF16); b_GT = S.buf("GT")
            VT = self.sb(ph, "VT", [128, 2304], BF16); b_VT = S.buf("VT")
            VTOK = self.sb(ph, "VTOK", [128, 33, 128], BF16); b_VTOK = S.buf("VTOK")
            ones = self.sb(ph, "ones", [128, 128], BF16); b_ones = S.buf("ones")
            Ex = [self.sb(ph, f"Ex{i}", [128, 384], BF16) for i in range(2)]
            b_Ex = S.bufs(2, "Ex")
            tmp = [self.sb(ph, f"tmp{i}", [128, 256], F32) for i in range(2)]
            b_tmp = S.bufs(2, "tmp")
            rden = [self.sb(ph, f"rden{i}", [128, 128], F32) for i in range(2)]
            b_rden = S.bufs(2, "rden")
            t2 = [self.sb(ph, f"t2{i}", [128, 128], F32) for i in range(2)]
            b_t2 = S.bufs(2, "t2")
            ATst = [self.sb(ph, f"ATn{i}", [128, 2304], BF16) for i in range(2)]
            b_ATst = S.bufs(2, "ATn")
            pin = [self.ps(ph, f"npin{i}", [128, 512], F32) for i in range(2)]
            b_pin = S.bufs(2, "npin")
            pst = [self.ps(ph, f"pst{i}", [128, 512], F32) for i in range(2)]
            b_pst = S.bufs(2, "pst")
            pov = [self.ps(ph, f"pov{i}", [128, 512], F32) for i in range(2)]
            b_pov = S.bufs(2, "pov")
            pvt = [self.ps(ph, f"pvt{i}", [128, 8, 128], BF16) for i in range(2)]
            b_pvt = S.bufs(2, "pvt")

            S.op("pool", lambda e: e.memset(ones[:], 1.0), writes=[b_ones])
            for pos in range(18):
                S.dma("sp", lambda e, pos=pos: e.dma_start(out=HT[:, pos], in_=hT_d[pos]), reads=[self.b_hT], writes=[b_HT[pos]])
            nin = 0
            nun = 0
            nvt = 0
            for hh in range(16):
                w, bw = wch[hh % 2], b_wch[hh % 2]
                u, bu = U[hh % 2], b_U[hh % 2]
                at, bat = ATst[hh % 2], b_ATst[hh % 2]
                S.dma("pool", lambda e, w=w, hh=hh: e.dma_start(out=w[:], in_=wv[:, :, hh * 512:(hh + 1) * 512]), writes=[bw])
                S.dma("sp", lambda e, u=u, hh=hh: e.dma_start(out=u[:], in_=utab_d[hh]), writes=[bu])
                for cb in range(4):
                    for (tok0, ntok, p0) in groups:
                        p, bp = pin[nin % 2], b_pin[nin % 2]
                        nin += 1
                        if p0 == "ctx":
                            rds = [b_HT[0], b_HT[9]]
                        else:
                            rds = [b_HT[p0 + i] for i in range(4)]
                        for kt in range(NKT):
                            if p0 == "ctx":
                                rhs = HT[:, 0:18:9, kt, :]
                            else:
                                rhs = HT[:, p0:p0 + 4, kt, :]
                            S.op("pe", lambda e, p=p, w=w, cb=cb, kt=kt, rhs=rhs, ntok=ntok: e.matmul(
                                p[:, 0:ntok], lhsT=w[:, kt, cb * 128:(cb + 1) * 128], rhs=rhs, start=(kt == 0), stop=(kt == NKT - 1)),
                                reads=rds + [bw], writes=[bp])
                        if cb == 0:
                            S.op("act", lambda e, p=p, tok0=tok0, ntok=ntok: e.copy(out=QT[:, tok0:tok0 + ntok], in_=p[:, 0:ntok]),
                                 reads=[bp], writes=[b_QT])
                        elif cb == 1:
                            S.op("dve", lambda e, p=p, tok0=tok0, ntok=ntok: e.tensor_copy(out=KT[:, tok0:tok0 + ntok], in_=p[:, 0:ntok]),
                                 reads=[bp], writes=[b_KT])
                        elif cb == 2:
                            S.op("dve", lambda e, p=p, tok0=tok0, ntok=ntok: e.tensor_copy(out=VT[:, tok0:tok0 + ntok], in_=p[:, 0:ntok]),
                                 reads=[bp], writes=[b_VT])
                        else:
                            S.op("act", lambda e, p=p, tok0=tok0, ntok=ntok: e.activation(out=GT[:, tok0:tok0 + ntok], in_=p[:, 0:ntok], func=AF.Silu),
                                 reads=[bp], writes=[b_GT])
                offs = [128 * g for g in range(18)] + [256 + 64 + 128 * m for m in range(15)]
                for b0 in range(0, 33, 8):
                    nb = min(8, 33 - b0)
                    pv, bpv = pvt[nvt % 2], b_pvt[nvt % 2]
                    nvt += 1
                    for k in range(nb):
                        o_ = offs[b0 + k]
                        S.op("pe", lambda e, pv=pv, k=k, o_=o_: e.transpose(out=pv[:, k, :], in_=VT[:, o_:o_ + 128], identity=self.ident[:]),
                             reads=[b_VT, self.b_ident], writes=[bpv])
                    S.op("act", lambda e, pv=pv, b0=b0, nb=nb: e.copy(out=VTOK[:, b0:b0 + nb, :], in_=pv[:, 0:nb, :]),
                         reads=[bpv], writes=[b_VTOK])
                units = []
                for cg in range(2):
                    units.append(dict(nq=128, qtok=128 * cg, ktiles=[(0, 0), (128, 1)], nloc=0, dr0=0, pos=g2p(cg), poff=0))
                for a in range(32):
                    rs = min(max(a - 4, 0), 24)
                    kts = []
                    for t in range(4):
                        ktok = 256 + 64 * rs + 128 * t
                        vi = (2 + rs // 2 + t) if rs % 2 == 0 else (18 + (rs - 1) // 2 + t)
                        kts.append((ktok, vi))
                    kts += [(0, 0), (128, 1)]
                    units.append(dict(nq=64, qtok=256 + 64 * a, ktiles=kts, nloc=4, dr0=rs - a + 7, pos=g2p(2 + a // 2), poff=(a % 2) * 64))
                for un in units:
                    i2 = nun % 2
                    nun += 1
                    nq, qtok, kts, nloc, dr0 = un["nq"], un["qtok"], un["ktiles"], un["nloc"], un["dr0"]
                    st_, bst = pst[i2], b_pst[i2]
                    nk = len(kts)
                    for t, (ktok, vi) in enumerate(kts):
                        S.op("pe", lambda e, st_=st_, t=t, ktok=ktok, qtok=qtok, nq=nq: e.matmul(
                            st_[:, t * nq:(t + 1) * nq], lhsT=KT[:, ktok:ktok + 128], rhs=QT[:, qtok:qtok + nq], start=True, stop=True),
                            reads=[b_KT, b_QT], writes=[bst])
                    ex, bex = Ex[i2], b_Ex[i2]
                    if nloc:
                        tm, btm = tmp[i2], b_tmp[i2]
                        S.op("dve", lambda e, tm=tm, st_=st_, u=u, dr0=dr0: e.scalar_tensor_tensor(
                            out=tm[:].rearrange("p (t q) -> p t q", t=4), in0=st_[:, 0:256].rearrange("p (t q) -> p t q", t=4),
                            scalar=NA_SCALE, in1=u[:, dr0:dr0 + 7:2, :], op0=ALU.mult, op1=ALU.add),
                            reads=[bst, bu], writes=[btm])
                        S.op("act", lambda e, ex=ex, tm=tm: e.activation(out=ex[:, 0:256], in_=tm[:], func=AF.Exp),
                             reads=[btm], writes=[bex])
                        S.op("act", lambda e, ex=ex, st_=st_: e.activation(out=ex[:, 256:384], in_=st_[:, 256:384], func=AF.Exp, scale=NA_SCALE),
                             reads=[bst], writes=[bex])
                    else:
                        S.op("act", lambda e, ex=ex, st_=st_: e.activation(out=ex[:, 0:256], in_=st_[:, 0:256], func=AF.Exp, scale=NA_SCALE),
                             reads=[bst], writes=[bex])
                    ov, bov = pov[i2], b_pov[i2]
                    for t, (ktok, vi) in enumerate(kts):
                        S.op("pe", lambda e, ov=ov, t=t, vi=vi, ex=ex, nq=nq, nk=nk: e.matmul(
                            ov[:, 0:nq], lhsT=VTOK[:, vi, :], rhs=ex[:, t * nq:(t + 1) * nq], start=(t == 0), stop=(t == nk - 1)),
                            reads=[b_VTOK, bex], writes=[bov])
                    for t in range(nk):
                        S.op("pe", lambda e, ov=ov, t=t, ex=ex, nq=nq, nk=nk: e.matmul(
                            ov[:, 128:128 + nq], lhsT=ones[:], rhs=ex[:, t * nq:(t + 1) * nq], start=(t == 0), stop=(t == nk - 1)),
                            reads=[b_ones, bex], writes=[bov])
                    rd, brd = rden[i2], b_rden[i2]
                    tt, btt = t2[i2], b_t2[i2]
                    S.op("dve", lambda e, rd=rd, ov=ov, nq=nq: e.reciprocal(out=rd[:, 0:nq], in_=ov[:, 128:128 + nq]), reads=[bov], writes=[brd])
                    S.op("dve", lambda e, tt=tt, rd=rd, ov=ov, nq=nq: e.tensor_tensor(out=tt[:, 0:nq], in0=ov[:, 0:nq], in1=rd[:, 0:nq], op=ALU.mult),
                         reads=[bov, brd], writes=[btt])
                    ao = un["pos"] * 128 + un["poff"]
                    S.op("pool", lambda e, at=at, tt=tt, ao=ao, qtok=qtok, nq=nq: e.tensor_tensor(
                        out=at[:, ao:ao + nq], in0=tt[:, 0:nq], in1=GT[:, qtok:qtok + nq], op=ALU.mult),
                        reads=[btt, b_GT], writes=[bat])
                for r_ in range(2):
                    S.dma("sp", lambda e, at=at, r_=r_, hh=hh: e.dma_start(out=AT_d[r_, hh], in_=at[:, r_ * 1152:(r_ + 1) * 1152]),
                          reads=[bat], writes=[self.b_AT])
            S.emit()

    def phase_out(self, AT_d, wout_d, x_src, x_dst, gt_of_tile):
        S = self.S
        wv = wout_d.rearrange("(kt p) n -> p kt n", p=128)
        ATv = AT_d.rearrange("r k p t -> p (r k) t")
        with contextlib.ExitStack() as ph:
            ATs = self.sb(ph, "ATs", [128, 32, 1152], BF16)
            b_ATs = S.bufs(4, "ATs")
            wch = [self.sb(ph, f"ow{i}", [128, 32, 512], BF16) for i in range(2)]
            b_wch = S.bufs(2, "ow")
            xq = [self.sb(ph, f"xq{i}", [128, 512], F32) for i in range(3)]
            b_xq = S.bufs(3, "xq")
            yq = [self.sb(ph, f"yq{i}", [128, 512], F32) for i in range(2)]
            b_yq = S.bufs(2, "yq")
            py = [self.ps(ph, f"py{i}", [128, 512], F32) for i in range(2)]
            b_py = S.bufs(2, "py")
            for q4 in range(4):
                S.dma("sp", lambda e, q4=q4: e.dma_start(out=ATs[:, q4 * 8:(q4 + 1) * 8, :], in_=ATv[:, q4 * 8:(q4 + 1) * 8, :]),
                      reads=[self.b_AT], writes=[b_ATs[q4]])
            n = 0
            for cch in range(4):
                w, bw = wch[cch % 2], b_wch[cch % 2]
                for hf in range(2):
                    S.dma("pool", lambda e, w=w, cch=cch, hf=hf: e.dma_start(
                        out=w[:, hf * 16:(hf + 1) * 16, :], in_=wv[:, hf * 16:(hf + 1) * 16, cch * 512:(cch + 1) * 512]), writes=[bw])
                for t in range(9):
                    p, bp = py[n % 2], b_py[n % 2]
                    x_, bx = xq[n % 3], b_xq[n % 3]
                    y_, by = yq[n % 2], b_yq[n % 2]
                    n += 1
                    S.dma("sp", lambda e, x_=x_, t=t, cch=cch: e.dma_start(out=x_[:], in_=x_src(t)[:, cch * 512:(cch + 1) * 512]), writes=[bx])
                    for kt in range(32):
                        S.op("pe", lambda e, p=p, w=w, kt=kt, t=t: e.matmul(
                            p[:], lhsT=ATs[:, kt, t * 128:(t + 1) * 128], rhs=w[:, kt, :], start=(kt == 0), stop=(kt == 31)),
                            reads=[b_ATs[kt // 8], bw], writes=[bp])
                    gi = gt_of_tile(t)
                    S.op("dve", lambda e, y_=y_, p=p, gi=gi, cch=cch: e.tensor_tensor(
                        out=y_[:], in0=p[:], in1=self.M[gi][:, cch * 512:(cch + 1) * 512], op=ALU.mult),
                        reads=[bp, self.b_M[gi]], writes=[by])
                    S.op("pool", lambda e, y_=y_, x_=x_: e.tensor_tensor(out=x_[:], in0=x_[:], in1=y_[:], op=ALU.add),
                         reads=[by, bx], writes=[bx])
                    S.dma("sp", lambda e, x_=x_, t=t, cch=cch: e.dma_start(out=x_dst(t)[:, cch * 512:(cch + 1) * 512], in_=x_[:]),
                          reads=[bx], writes=[self.b_xd])
            S.emit()

    def phase_final(self, x_src, out_dst, fg_d, tiles):
        S = self.S
        with contextlib.ExitStack() as ph:
            fg = self.sb(ph, "fg", [128, D], F32)
            b_fg = S.buf()
            xt = [self.sb(ph, f"fx{i}", [128, D], F32) for i in range(2)]
            b_xt = S.bufs(2, "fx")
            junk = self.sb(ph, "fjunk", [128, D], BF16)
            b_junk = S.buf()
            ss = [self.sb(ph, f"fss{i}", [128, 1], F32) for i in range(2)]
            b_ss = S.bufs(2, "fss")
            yo = [self.sb(ph, f"fy{i}", [128, D], F32) for i in range(2)]
            b_yo = S.bufs(2, "fy")
            S.dma("sp", lambda e: e.dma_start(out=fg[:], in_=fg_d.partition_broadcast(128)), writes=[b_fg])
            for n, t in enumerate(tiles):
                i = n % 2
                S.dma("sp", lambda e, i=i, t=t: e.dma_start(out=xt[i][:], in_=x_src(t)), reads=[self.b_xd], writes=[b_xt[i]])
                S.op("act", lambda e, i=i: e.activation(out=junk[:], in_=xt[i][:], func=AF.Square, scale=float(D) ** -0.5, accum_out=ss[i][:]),
                     reads=[b_xt[i]], writes=[b_junk, b_ss[i]])
                S.op("dve", lambda e, i=i: e.tensor_scalar(out=ss[i][:], in0=ss[i][:], scalar1=EPS, scalar2=None, op0=ALU.add),
                     reads=[b_ss[i]], writes=[b_ss[i]])
                S.op("act", lambda e, i=i: e.sqrt(out=ss[i][:], in_=ss[i][:]), reads=[b_ss[i]], writes=[b_ss[i]])
                S.op("dve", lambda e, i=i: e.reciprocal(out=ss[i][:], in_=ss[i][:]), reads=[b_ss[i]], writes=[b_ss[i]])
                S.op("dve", lambda e, i=i: e.scalar_tensor_tensor(out=yo[i][:], in0=xt[i][:], scalar=ss[i][:, 0:1], in1=fg[:],
                                                                  op0=ALU.mult, op1=ALU.mult), reads=[b_xt[i], b_ss[i], b_fg], writes=[b_yo[i]])
                S.dma("sp", lambda e, i=i, n=n: e.dma_start(out=out_dst(n), in_=yo[i][:]), reads=[b_yo[i]], writes=[self.b_out])
            S.emit()


def build(mode):
    P = Prog(mode)
    nc, S = P.nc, P.S
    P.b_hT = S.buf("hT"); P.b_AT = S.buf("AT"); P.b_xd = S.buf("xd"); P.b_out = S.buf("out")
    cvec_d = P.din("cvec", [128, 2, NKT])
    const_d = P.din("const", [128, CW])
    modw_d = P.din("mod_w", [D, 3 * D])
    modb_d = P.din("mod_b", [1, 3 * D])
    if mode in ("A_ret", "A_na"):
        xfull_d = P.din("x_full", [18, 128, D])
        ng_d = P.din("norm_g", [1, D])
        AT_d = P.dout("AT", [2, 16, 128, 1152], BF16)
        hT_d = P.dscr("hT", [18, 128, NKT, 128], BF16)
        if mode == "A_ret":
            win_d = P.din("w_in", [D, 6144])
            dec_d = P.din("dec", [1, 8])
            cos_d = P.din("cos", [128, 16, 2, 64])
            sin_d = P.din("sin", [128, 16, 2, 64])
        else:
            win_d = P.din("w_in", [D, 8192])
            utab_d = P.din("utab", [16, 128, 14, 64])
        P.setup_persist(cvec_d, const_d)
        with contextlib.ExitStack() as mst:
            P.alloc_M(mst)
            P.phase_mod(modw_d, modb_d, 0, 2 * D, {(0, 0): 0, (0, 1): 1, (1, 0): 2, (1, 1): 3})
            P.phase_gs(ng_d, [1, 3])
            P.phase_norm(lambda pos: xfull_d[pos], list(range(18)), hT_d,
                         lambda pos: (2, 3) if pos in (0, 9) else (0, 1))
        if mode == "A_ret":
            P.phase_ret(hT_d, win_d, dec_d, cos_d, sin_d, AT_d)
        else:
            P.phase_na(hT_d, win_d, utab_d, AT_d)
    else:
        AT_d = P.din("ATr", [2, 16, 128, 1152], BF16)
        xown_d = P.din("x_own", [9, 128, D])
        wout_d = P.din("w_out", [E, D])
        xnew_d = P.dout("x_new", [9, 128, D])
        P.setup_persist(cvec_d, const_d)
        P.alloc_M(P.stack)
        P.phase_mod(modw_d, modb_d, 2 * D, D, {(0, 0): 0, (1, 0): 1})
        P.phase_out(AT_d, wout_d, lambda t: xown_d[t], lambda t: xnew_d[t], lambda t: 1 if t == 0 else 0)
        if mode == "B_last":
            fg_d = P.din("final_g", [1, D])
            out_d = P.dout("out", [8, 128, D])
            P.phase_final(lambda t: xnew_d[t], lambda n: out_d[n], fg_d, list(range(1, 9)))
    P.stack.close()
    return nc


_PROGS = {}
_DBG = None


def get_prog(mode):
    if mode not in _PROGS:
        _PROGS[mode] = build(mode)
    return _PROGS[mode]


def kernel(x, c, ctx, c_ctx, mod_w, mod_b, norm_g, ret_w_in, ret_decay_fwd, ret_decay_bwd, ret_w_out,
           na_w_in, na_rpb, na_w_out, final_g):
    f32 = np.float32
    x = np.asarray(x, f32); ctx = np.asarray(ctx, f32)
    const = make_const()
    cos, sin = make_rope()
    cores = [(b, s) for b in range(4) for s in range(2)]
    xo = []
    cvecs = []
    for (b, s) in cores:
        t = np.concatenate([ctx[b, 128 * s:128 * (s + 1)], x[b, 1024 * s:1024 * (s + 1)]], axis=0).reshape(9, 128, D)
        xo.append(np.ascontiguousarray(t))
        cv = np.stack([np.asarray(c[b], f32).reshape(NKT, 128).T, np.asarray(c_ctx, f32).reshape(NKT, 128).T], axis=1)
        cvecs.append(np.ascontiguousarray(cv))
    out = None
    for l in range(4):
        j = l // 2
        mw = np.ascontiguousarray(mod_w[l], dtype=f32)
        mb = np.ascontiguousarray(mod_b[l], dtype=f32).reshape(1, 3 * D)
        ng = np.ascontiguousarray(norm_g[l], dtype=f32).reshape(1, D)
        in_maps = []
        for ci, (b, s) in enumerate(cores):
            xfull = np.concatenate([xo[2 * b], xo[2 * b + 1]], axis=0)
            m = {"cvec": cvecs[ci], "const": const, "mod_w": mw, "mod_b": mb, "x_full": xfull, "norm_g": ng}
            if l % 2 == 0:
                W = ret_w_in[j]
                parts = []
                for h in range(4 * s, 4 * s + 4):
                    parts += [W[:, h * 256:(h + 1) * 256], W[:, 2048 + h * 256:2048 + (h + 1) * 256],
                              W[:, 4096 + h * 512:4096 + (h + 1) * 512], W[:, 8192 + h * 512:8192 + (h + 1) * 512]]
                m["w_in"] = np.ascontiguousarray(np.concatenate(parts, axis=1), dtype=f32)
                m["dec"] = np.concatenate([ret_decay_fwd[j][4 * s:4 * s + 4], ret_decay_bwd[j][4 * s:4 * s + 4]]).astype(f32).reshape(1, 8)
                m["cos"] = cos; m["sin"] = sin
            else:
                W = na_w_in[j]
                parts = []
                for h in range(16 * s, 16 * s + 16):
                    parts += [W[:, k * 4096 + h * 128:k * 4096 + (h + 1) * 128] for k in range(4)]
                m["w_in"] = np.ascontiguousarray(np.concatenate(parts, axis=1), dtype=f32)
                m["utab"] = make_utab(np.asarray(na_rpb[j][16 * s:16 * s + 16], f32))
            in_maps.append(m)
        nc = get_prog("A_ret" if l % 2 == 0 else "A_na")
        res = run_bass_kernel_spmd(nc, in_maps, core_ids=list(range(8)))
        ATs = [r["AT"] for r in res.results]
        del in_maps
        wout = np.ascontiguousarray((ret_w_out if l % 2 == 0 else na_w_out)[j], dtype=f32)
        in_maps = []
        for ci, (b, s) in enumerate(cores):
            atr = np.stack([ATs[2 * b][s], ATs[2 * b + 1][s]], axis=0)
            m = {"cvec": cvecs[ci], "const": const, "mod_w": mw, "mod_b": mb, "ATr": np.ascontiguousarray(atr),
                 "x_own": xo[ci], "w_out": wout}
            if l == 3:
                m["final_g"] = np.asarray(final_g, f32).reshape(1, D)
            in_maps.append(m)
        nc = get_prog("B_last" if l == 3 else "B")
        res = run_bass_kernel_spmd(nc, in_maps, core_ids=list(range(8)))
        xo = [np.asarray(r["x_new"]) for r in res.results]
        if _DBG is not None:
            _DBG(l, xo, ATs)
        if l == 3:
            out = np.empty((4, T, D), f32)
            for ci, (b, s) in enumerate(cores):
                out[b, 1024 * s:1024 * (s + 1)] = np.asarray(res.results[ci]["out"]).reshape(1024, D)
    return out
```

```python
import contextlib
import numpy as np
import ml_dtypes
import concourse.bass as bass
import concourse.mybir as mybir
from concourse.bass_utils import run_bass_kernel_spmd

F32 = mybir.dt.float32
BF16 = mybir.dt.bfloat16
ALU = mybir.AluOpType
AF = mybir.ActivationFunctionType

D = 2048
E = 4096
T = 2048
TC = 256
NKT = 16
EPS = 1e-6
MASKVAL = -30000.0
NA_SCALE = 128 ** -0.5
RET_KSCALE = 256 ** -0.5

ENGS = ["pe", "act", "dve", "pool", "sp"]
N_DMA_SEMS = 24


def g2p(g):
    if g == 0:
        return 0
    if g == 1:
        return 9
    c = g - 2
    return 1 + c if c < 8 else 10 + (c - 8)


class Buf:
    __slots__ = ("name", "w", "r")

    def __init__(self, name):
        self.name = name
        self.w = None
        self.r = []


class Sched:
    def __init__(self, nc, stack):
        self.nc = nc
        self.esem = {e: stack.enter_context(nc.semaphore(f"s_{e}")) for e in ENGS}
        self.ebase = {e: 0 for e in ENGS}
        self.dsem = {}
        self.dsem_val = {}
        self.dsem_next = {}
        for q in ["sp", "pool", "act"]:
            self.dsem[q] = [stack.enter_context(nc.semaphore(f"d_{q}{i}")) for i in range(N_DMA_SEMS)]
            self.dsem_val[q] = [0] * N_DMA_SEMS
            self.dsem_next[q] = 0
        self.epoch = 0
        self._reset()

    def _reset(self):
        self.ops = {e: [] for e in ENGS}
        self.waited = {e: {} for e in ENGS}
        self.dma_issued = []

    def buf(self, name="b"):
        return Buf(name)

    def bufs(self, n, name="b"):
        return [Buf(f"{name}{i}") for i in range(n)]

    def _live(self, t):
        return t is not None and t[-1] == self.epoch

    def _deps(self, eng, reads, writes):
        toks = []
        for b in reads:
            if self._live(b.w):
                toks.append(b.w)
        for b in writes:
            if self._live(b.w):
                if not (b.w[0] == "eng" and b.w[1] == eng):
                    toks.append(b.w)
            for t in b.r:
                if not self._live(t):
                    continue
                if t[0] == "eng" and t[1] == eng:
                    continue
                toks.append(t)
        best = {}
        for t in toks:
            if t[0] == "eng":
                key = ("eng", t[1]); val = t[2]
            else:
                key = ("dma", t[1], t[2]); val = t[3]
            if key not in best or best[key][0] < val:
                best[key] = (val, t)
        out = []
        wd = self.waited[eng]
        for key, (val, t) in best.items():
            if wd.get(key, -1) >= val:
                continue
            wd[key] = val
            out.append(t)
        return out

    def _mark(self, tok, reads, writes):
        for b in reads:
            if b.r and not self._live(b.r[0]):
                b.r = []
            b.r.append(tok)
        for b in writes:
            b.w = tok
            b.r = []

    def op(self, eng, fn, reads=(), writes=()):
        waits = self._deps(eng, reads, writes)
        idx = len(self.ops[eng])
        self.ops[eng].append(dict(kind="op", fn=fn, waits=waits, sig=False, idx=idx))
        tok = ("eng", eng, idx, self.epoch)
        self._mark(tok, reads, writes)
        return tok

    def dma(self, q, fn, reads=(), writes=()):
        waits = self._deps(q, reads, writes)
        k = self.dsem_next[q]
        self.dsem_next[q] = (k + 1) % N_DMA_SEMS
        prev = self.dsem_val[q][k]
        if prev > 0:
            key = ("dma", q, k)
            if self.waited[q].get(key, -1) < prev:
                self.waited[q][key] = prev
                waits.append(("dma", q, k, prev, self.epoch))
        val = prev + 16
        self.dsem_val[q][k] = val
        idx = len(self.ops[q])
        self.ops[q].append(dict(kind="dma", fn=fn, waits=waits, sig=False, idx=idx, dsem=(q, k)))
        tok = ("dma", q, k, val, self.epoch)
        self.dma_issued.append(tok)
        self._mark(tok, reads, writes)
        return tok

    def emit(self):
        nc = self.nc
        last = {}
        for t in self.dma_issued:
            last[(t[1], t[2])] = t
        for (q, k), t in last.items():
            if self.waited[q].get(("dma", q, k), -1) < t[3]:
                self.ops[q].append(dict(kind="wait", fn=None, waits=[t], sig=False, idx=len(self.ops[q])))
        for e in ENGS:
            for rec in self.ops[e]:
                for t in rec["waits"]:
                    if t[0] == "eng":
                        self.ops[t[1]][t[2]]["sig"] = True
        semval = {}
        for e in ENGS:
            c = self.ebase[e]
            for rec in self.ops[e]:
                if rec["kind"] == "op" and rec["sig"]:
                    c += 1
                    semval[(e, rec["idx"])] = c
            self.ebase[e] = c

        def make_body(e):
            ops = self.ops[e]

            def body(engine):
                for rec in ops:
                    for t in rec["waits"]:
                        if t[0] == "eng":
                            engine.wait_ge(self.esem[t[1]], semval[(t[1], t[2])])
                        else:
                            engine.wait_ge(self.dsem[t[1]][t[2]], t[3])
                    if rec["kind"] == "op":
                        ins = rec["fn"](engine)
                        if rec["sig"]:
                            ins.then_inc(self.esem[e], 1)
                    elif rec["kind"] == "dma":
                        ins = rec["fn"](engine)
                        q, k = rec["dsem"]
                        ins.then_inc(self.dsem[q][k], 16)
            return body

        with nc.Block() as block:
            block.tensor(make_body("pe"))
            block.scalar(make_body("act"))
            block.vector(make_body("dve"))
            block.gpsimd(make_body("pool"))
            block.sync(make_body("sp"))
        self.epoch += 1
        self._reset()


C_IDENT = 0
C_DPOS = 128
C_DNEG = 256
C_MLO = 384
C_MUP = 512
C_I1 = 640
C_I2 = 768
C_P1 = 896
C_P2 = 897
C_128 = 898
CW = 900


def make_const():
    c = np.zeros((128, CW), np.float32)
    j = np.arange(128)[:, None].astype(np.float32)
    i = np.arange(128)[None, :].astype(np.float32)
    c[:, C_IDENT:C_IDENT + 128] = np.eye(128, dtype=np.float32)
    c[:, C_DPOS:C_DPOS + 128] = np.maximum(i - j, 0)
    c[:, C_DNEG:C_DNEG + 128] = np.maximum(j - i, 0)
    c[:, C_MLO:C_MLO + 128] = (i >= j)
    c[:, C_MUP:C_MUP + 128] = (j >= i)
    c[:, C_I1:C_I1 + 128] = i + 1
    c[:, C_I2:C_I2 + 128] = 128 - i
    c[:, C_P1] = 127 - np.arange(128)
    c[:, C_P2] = np.arange(128)
    c[:, C_128] = 128.0
    return c


def make_rope():
    half = 64
    freqs = (np.float32(10000.0) ** (-np.arange(half, dtype=np.float32) / np.float32(half))).astype(np.float32)
    t = np.arange(T)
    row = (t // 64).astype(np.float32)
    col = (t % 64).astype(np.float32)
    ang_r = (row[:, None] * freqs[None, :]).astype(np.float32)
    ang_c = (col[:, None] * freqs[None, :]).astype(np.float32)
    cos = np.stack([np.cos(ang_r), np.cos(ang_c)], axis=1).astype(np.float32)
    sin = np.stack([np.sin(ang_r), np.sin(ang_c)], axis=1).astype(np.float32)
    cos = cos.reshape(16, 128, 2, 64).transpose(1, 0, 2, 3)
    sin = sin.reshape(16, 128, 2, 64).transpose(1, 0, 2, 3)
    return np.ascontiguousarray(cos), np.ascontiguousarray(sin)


def make_utab(rpb_h):
    qc = np.arange(64)[None, :]
    kc = np.arange(64)[:, None]
    ws = np.clip(qc - 8, 0, 48)
    valid = (kc >= ws) & (kc < ws + 16)
    dc = np.clip(kc - qc + 15, 0, 30)
    nh = rpb_h.shape[0]
    U = np.empty((nh, 128, 14, 64), np.float32)
    for jj in range(2):
        for e in range(14):
            dr = e + jj
            vals = rpb_h[:, dr][:, dc]
            U[:, jj * 64:(jj + 1) * 64, e, :] = np.where(valid[None], vals, np.float32(MASKVAL))
    return U


class Prog:
    def __init__(self, mode):
        self.mode = mode
        self.nc = bass.Bass("TRN2", target_bir_lowering=False)
        if mode == "fused" and not DGE_PRECOOK:
            self.nc.dge_precook = False
        self.stack = contextlib.ExitStack()
        self.S = Sched(self.nc, self.stack)

    def din(self, name, shape, dt=F32):
        return self.nc.dram_tensor(name, list(shape), dt, kind="ExternalInput").ap()

    def dout(self, name, shape, dt=F32):
        return self.nc.dram_tensor(name, list(shape), dt, kind="ExternalOutput").ap()

    def dscr(self, name, shape, dt):
        return self.nc.dram_tensor(name, list(shape), dt).ap()

    def sb(self, st, name, shape, dt):
        self._n = getattr(self, "_n", 0) + 1
        return st.enter_context(self.nc.sbuf_tensor(f"sb{self._n}_{name}", list(shape), dt))

    def ps(self, st, name, shape, dt):
        self._n = getattr(self, "_n", 0) + 1
        return st.enter_context(self.nc.psum_tensor(f"ps{self._n}_{name}", list(shape), dt))

    def setup_persist(self, cvec_d, const_d):
        S, st = self.S, self.stack
        self.const = self.sb(st, "const", [128, CW], F32)
        self.b_const = S.buf("const")
        self.ident = self.sb(st, "ident", [128, 128], BF16)
        self.b_ident = S.buf("ident")
        self.condB = self.sb(st, "condB", [128, 2, NKT, 128], BF16)
        self.b_condB = S.buf("condB")
        with contextlib.ExitStack() as ph:
            cv = self.sb(ph, "cv", [128, 2, NKT], F32)
            cs = self.sb(ph, "cs", [128, 2, NKT], F32)
            b_cv, b_cs = S.bufs(2)
            S.dma("sp", lambda e: e.dma_start(out=self.const[:], in_=const_d), writes=[self.b_const])
            S.dma("sp", lambda e: e.dma_start(out=cv[:], in_=cvec_d), writes=[b_cv])
            S.op("dve", lambda e: e.tensor_copy(out=self.ident[:], in_=self.const[:, C_IDENT:C_IDENT + 128]),
                 reads=[self.b_const], writes=[self.b_ident])
            S.op("act", lambda e: e.activation(out=cs[:], in_=cv[:], func=AF.Silu), reads=[b_cv], writes=[b_cs])
            for c in range(2):
                S.op("dve", lambda e, c=c: e.tensor_copy(
                    out=self.condB[:, c], in_=cs[:, c, :].unsqueeze(2).to_broadcast([128, NKT, 128])),
                    reads=[b_cs], writes=[self.b_condB])
            S.emit()

    def alloc_M(self, st):
        self.M = [self.sb(st, f"M{i}", [128, D], F32) for i in range(4)]
        self.b_M = self.S.bufs(4, "M")

    def phase_mod(self, modw_d, modb_d, col0, ncols, dests):
        S = self.S
        nchunk = ncols // 512
        with contextlib.ExitStack() as ph:
            wch = [self.sb(ph, f"mw{i}", [128, NKT, 512], BF16) for i in range(2)]
            b_wch = S.bufs(2, "mw")
            mb = [self.sb(ph, f"mb{i}", [128, 512], F32) for i in range(2)]
            b_mb = S.bufs(2, "mb")
            pp = [self.ps(ph, f"pm{i}", [128, 512], F32) for i in range(4)]
            b_pp = S.bufs(4, "pm")
            wv = modw_d.rearrange("(kt p) n -> p kt n", p=128)
            n = 0
            for ch in range(nchunk):
                cc = col0 + ch * 512
                w, bw = wch[ch % 2], b_wch[ch % 2]
                S.dma("pool", lambda e, w=w, cc=cc: e.dma_start(out=w[:], in_=wv[:, :, cc:cc + 512]), writes=[bw])
                m, bm = mb[ch % 2], b_mb[ch % 2]
                S.dma("sp", lambda e, m=m, cc=cc: e.dma_start(out=m[:], in_=modb_d[:, cc:cc + 512].partition_broadcast(128)),
                      writes=[bm])
                blk = (ch * 512) // D
                off = (ch * 512) % D
                for c in range(2):
                    if (c, blk) not in dests:
                        continue
                    p, bp = pp[n % 4], b_pp[n % 4]
                    n += 1
                    for kt in range(NKT):
                        S.op("pe", lambda e, p=p, c=c, kt=kt, w=w: e.matmul(
                            p[:], lhsT=self.condB[:, c, kt, :], rhs=w[:, kt, :], start=(kt == 0), stop=(kt == NKT - 1)),
                            reads=[self.b_condB, bw], writes=[bp])
                    mi = dests[(c, blk)]
                    S.op("dve", lambda e, p=p, m=m, mi=mi, off=off: e.tensor_tensor(
                        out=self.M[mi][:, off:off + 512], in0=p[:], in1=m[:], op=ALU.add),
                        reads=[bp, bm], writes=[self.b_M[mi]])
            S.emit()

    def phase_gs(self, ng_d, pairs):
        S = self.S
        with contextlib.ExitStack() as ph:
            ng = self.sb(ph, "ng", [128, D], F32)
            b_ng = S.buf()
            S.dma("sp", lambda e: e.dma_start(out=ng[:], in_=ng_d.partition_broadcast(128)), writes=[b_ng])
            for mi in pairs:
                S.op("dve", lambda e, mi=mi: e.scalar_tensor_tensor(
                    out=self.M[mi][:], in0=self.M[mi][:], scalar=1.0, in1=ng[:], op0=ALU.add, op1=ALU.mult),
                    reads=[self.b_M[mi], b_ng], writes=[self.b_M[mi]])
            S.emit()

    def phase_norm(self, x_src, positions, hT_d, mod_of_pos):
        S = self.S
        with contextlib.ExitStack() as ph:
            xt = [self.sb(ph, f"xt{i}", [128, D], F32) for i in range(2)]
            b_xt = S.bufs(2, "xt")
            junk = self.sb(ph, "junk", [128, D], BF16)
            b_junk = S.buf()
            ss = [self.sb(ph, f"ss{i}", [128, 1], F32) for i in range(2)]
            b_ss = S.bufs(2, "ss")
            t1 = [self.sb(ph, f"t1{i}", [128, D], F32) for i in range(2)]
            b_t1 = S.bufs(2, "t1")
            hb = [self.sb(ph, f"hb{i}", [128, D], BF16) for i in range(2)]
            b_hb = S.bufs(2, "hb")
            hs = [self.sb(ph, f"hs{i}", [128, NKT, 128], BF16) for i in range(2)]
            b_hs = S.bufs(2, "hs")
            pt = [self.ps(ph, f"pt{i}", [128, 8, 128], BF16) for i in range(4)]
            b_pt = S.bufs(4, "pt")
            for n, pos in enumerate(positions):
                i = n % 2
                shi, gsi = mod_of_pos(pos)
                S.dma("sp", lambda e, i=i, pos=pos: e.dma_start(out=xt[i][:], in_=x_src(pos)), writes=[b_xt[i]])
                S.op("act", lambda e, i=i: e.activation(out=junk[:], in_=xt[i][:], func=AF.Square, scale=float(D) ** -0.5, accum_out=ss[i][:]),
                     reads=[b_xt[i]], writes=[b_junk, b_ss[i]])
                S.op("dve", lambda e, i=i: e.tensor_scalar(out=ss[i][:], in0=ss[i][:], scalar1=EPS, scalar2=None, op0=ALU.add),
                     reads=[b_ss[i]], writes=[b_ss[i]])
                S.op("act", lambda e, i=i: e.sqrt(out=ss[i][:], in_=ss[i][:]), reads=[b_ss[i]], writes=[b_ss[i]])
                S.op("dve", lambda e, i=i: e.reciprocal(out=ss[i][:], in_=ss[i][:]), reads=[b_ss[i]], writes=[b_ss[i]])
                S.op("dve", lambda e, i=i, gsi=gsi: e.scalar_tensor_tensor(
                    out=t1[i][:], in0=xt[i][:], scalar=ss[i][:, 0:1], in1=self.M[gsi][:], op0=ALU.mult, op1=ALU.mult),
                    reads=[b_xt[i], b_ss[i], self.b_M[gsi]], writes=[b_t1[i]])
                S.op("pool", lambda e, i=i, shi=shi: e.tensor_tensor(out=hb[i][:], in0=t1[i][:], in1=self.M[shi][:], op=ALU.add),
                     reads=[b_t1[i], self.b_M[shi]], writes=[b_hb[i]])
                for half in range(2):
                    pi = (2 * n + half) % 4
                    for k8 in range(8):
                        kt = half * 8 + k8
                        S.op("pe", lambda e, i=i, pi=pi, k8=k8, kt=kt: e.transpose(
                            out=pt[pi][:, k8, :], in_=hb[i][:, kt * 128:(kt + 1) * 128], identity=self.ident[:]),
                            reads=[b_hb[i], self.b_ident], writes=[b_pt[pi]])
                    eng = "act" if half == 0 else "dve"
                    if eng == "act":
                        S.op("act", lambda e, i=i, pi=pi, half=half: e.copy(out=hs[i][:, half * 8:(half + 1) * 8, :], in_=pt[pi][:]),
                             reads=[b_pt[pi]], writes=[b_hs[i]])
                    else:
                        S.op("dve", lambda e, i=i, pi=pi, half=half: e.tensor_copy(out=hs[i][:, half * 8:(half + 1) * 8, :], in_=pt[pi][:]),
                             reads=[b_pt[pi]], writes=[b_hs[i]])
                S.dma("sp", lambda e, i=i, pos=pos: e.dma_start(out=hT_d[pos], in_=hs[i][:]), reads=[b_hs[i]], writes=[self.b_hT])
            S.emit()

    def phase_ret(self, hT_d, win_d, dec_d, cos_d, sin_d, AT_d):
        S = self.S
        order_b = [1, 0] + list(range(17, 1, -1))
        wv = win_d.rearrange("(kt p) n -> p kt n", p=128)
        with contextlib.ExitStack() as ph:
            wch = [self.sb(ph, f"rw{i}", [128, NKT, 512], BF16) for i in range(2)]
            b_wch = S.bufs(2, "rw")
            hts = [self.sb(ph, f"hts{i}", [128, NKT, 128], BF16) for i in range(3)]
            b_hts = S.bufs(3, "hts")
            QK = self.sb(ph, "QK", [128, 18, 512], BF16)
            V = self.sb(ph, "V", [128, 18, 512], BF16)
            G = self.sb(ph, "G", [128, 18, 512], BF16)
            b_QK = S.bufs(18, "QK"); b_V = S.bufs(18, "V"); b_G = S.bufs(18, "G")
            SBS = self.sb(ph, "SBS", [128, 18, 2, 512], BF16)
            b_SBS = S.bufs(18, "SBS")
            COS = self.sb(ph, "COS", [128, 16, 2, 64], F32)
            SIN = self.sb(ph, "SIN", [128, 16, 2, 64], F32)
            b_rope = S.buf("rope")
            raw = [self.sb(ph, f"raw{i}", [128, 512], F32) for i in range(2)]
            b_raw = S.bufs(2, "raw")
            rt = [self.sb(ph, f"rt{i}", [128, 2, 2, 64], F32) for i in range(4)]
            b_rt = S.bufs(4, "rt")
            Sst = [self.sb(ph, f"Sst{i}", [128, 2, 512], F32) for i in range(2)]
            b_Sst = S.bufs(2, "Sst")
            sfb = [self.sb(ph, f"sfb{i}", [128, 2, 512], BF16) for i in range(2)]
            b_sfb = S.bufs(2, "sfb")
            ksc = [self.sb(ph, f"ksc{i}", [128, 256], BF16) for i in range(2)]
            b_ksc = S.bufs(2, "ksc")
            qkT = [self.sb(ph, f"qkT{i}", [128, 4, 128], BF16) for i in range(2)]
            b_qkT = S.bufs(2, "qkT")
            qfT = [self.sb(ph, f"qfT{i}", [128, 2, 128], BF16) for i in range(2)]
            b_qfT = S.bufs(2, "qfT")
            qbT = [self.sb(ph, f"qbT{i}", [128, 2, 128], BF16) for i in range(2)]
            b_qbT = S.bufs(2, "qbT")
            Pm = [self.sb(ph, f"Pm{i}", [128, 128], BF16) for i in range(2)]
            b_Pm = S.bufs(2, "Pm")
            junk = self.sb(ph, "rjunk", [128, 512], BF16)
            b_junk = S.buf()
            hss = [self.sb(ph, f"hss{i}", [128, 1], F32) for i in range(2)]
            b_hss = S.bufs(2, "hss")
            Atok = [self.sb(ph, f"Atok{i}", [128, 512], BF16) for i in range(2)]
            b_Atok = S.bufs(2, "Atok")
            ATst = [self.sb(ph, f"ATst{i}", [128, 4, 128], BF16) for i in range(2)]
            b_ATst = S.bufs(2, "ATst")
            dec = self.sb(ph, "dec", [128, 8], F32)
            lg = self.sb(ph, "lg", [128, 8], F32)
            b_dec, b_lg = S.bufs(2)
            Mt = self.sb(ph, "Mt", [128, 128], F32)
            Mt2 = self.sb(ph, "Mt2", [128, 128], F32)
            QDF = self.sb(ph, "QDF", [128, 128], BF16)
            QDB = self.sb(ph, "QDB", [128, 128], BF16)
            KD = self.sb(ph, "KD", [128, 4], F32)
            b_tab = S.buf("tab")
            pin = [self.ps(ph, f"pin{i}", [128, 512], F32) for i in range(2)]
            b_pin = S.bufs(2, "pin")
            pS = [self.ps(ph, f"pS{i}", [128, 512], F32) for i in range(2)]
            b_pS = S.bufs(2, "pS")
            po = [self.ps(ph, f"po{i}", [128, 512], F32) for i in range(2)]
            b_po = S.bufs(2, "po")
            psc_full = self.ps(ph, "psc", [128, 512], F32)
            psc = psc_full[:, 0:128]
            b_psc = S.buf()
            ptr = self.ps(ph, "ptr", [128, 8, 128], BF16)
            b_ptrq = S.buf()
            ptra = pin[0][:].bitcast(BF16).rearrange("p (a b) -> p a b", b=128)
            b_ptra = b_pin[0]

            S.dma("sp", lambda e: e.dma_start(out=COS[:], in_=cos_d), writes=[b_rope])
            S.dma("sp", lambda e: e.dma_start(out=SIN[:], in_=sin_d), writes=[b_rope])
            S.dma("sp", lambda e: e.dma_start(out=dec[:], in_=dec_d.partition_broadcast(128)), writes=[b_dec])
            S.op("act", lambda e: e.activation(out=lg[:], in_=dec[:], func=AF.Exp), reads=[b_dec], writes=[b_lg])
            S.op("dve", lambda e: e.tensor_scalar(out=lg[:], in0=lg[:], scalar1=1.0, scalar2=None, op0=ALU.add),
                 reads=[b_lg], writes=[b_lg])
            S.op("act", lambda e: e.activation(out=lg[:], in_=lg[:], func=AF.Ln), reads=[b_lg], writes=[b_lg])
            S.op("dve", lambda e: e.tensor_scalar(out=lg[:], in0=lg[:], scalar1=-1.0, scalar2=None, op0=ALU.mult),
                 reads=[b_lg], writes=[b_lg])
            C = self.const
            nin = 0
            nh = 0
            for hh in range(4):
                lgf = lg[:, hh:hh + 1]
                lgb = lg[:, 4 + hh:5 + hh]
                S.op("act", lambda e, lgf=lgf: e.activation(out=Mt[:], in_=C[:, C_DPOS:C_DPOS + 128], func=AF.Exp, scale=lgf),
                     reads=[b_lg, self.b_const], writes=[b_tab])
                S.op("act", lambda e, lgb=lgb: e.activation(out=Mt2[:], in_=C[:, C_DNEG:C_DNEG + 128], func=AF.Exp, scale=lgb),
                     reads=[b_lg, self.b_const], writes=[b_tab])
                S.op("dve", lambda e: e.tensor_tensor(out=Mt[:], in0=Mt[:], in1=C[:, C_MLO:C_MLO + 128], op=ALU.mult),
                     reads=[b_tab], writes=[b_tab])
                S.op("dve", lambda e: e.tensor_tensor(out=Mt2[:], in0=Mt2[:], in1=C[:, C_MUP:C_MUP + 128], op=ALU.mult),
                     reads=[b_tab], writes=[b_tab])
                S.op("dve", lambda e: e.tensor_tensor(out=Mt[:], in0=Mt[:], in1=Mt2[:], op=ALU.add), reads=[b_tab], writes=[b_tab])
                S.op("dve", lambda e: e.tensor_scalar(out=Mt[:], in0=Mt[:], scalar1=RET_KSCALE, scalar2=None, op0=ALU.mult),
                     reads=[b_tab], writes=[b_tab])
                S.op("act", lambda e, lgf=lgf: e.activation(out=QDF[:], in_=C[:, C_I1:C_I1 + 128], func=AF.Exp, scale=lgf),
                     reads=[b_lg, self.b_const], writes=[b_tab])
                S.op("act", lambda e, lgb=lgb: e.activation(out=QDB[:], in_=C[:, C_I2:C_I2 + 128], func=AF.Exp, scale=lgb),
                     reads=[b_lg, self.b_const], writes=[b_tab])
                S.op("act", lambda e, lgf=lgf: e.activation(out=KD[:, 0:1], in_=C[:, C_P1:C_P1 + 1], func=AF.Exp, scale=lgf),
                     reads=[b_lg, self.b_const], writes=[b_tab])
                S.op("act", lambda e, lgb=lgb: e.activation(out=KD[:, 1:2], in_=C[:, C_P2:C_P2 + 1], func=AF.Exp, scale=lgb),
                     reads=[b_lg, self.b_const], writes=[b_tab])
                S.op("act", lambda e, lgf=lgf: e.activation(out=KD[:, 2:3], in_=C[:, C_128:C_128 + 1], func=AF.Exp, scale=lgf),
                     reads=[b_lg, self.b_const], writes=[b_tab])
                S.op("act", lambda e, lgb=lgb: e.activation(out=KD[:, 3:4], in_=C[:, C_128:C_128 + 1], func=AF.Exp, scale=lgb),
                     reads=[b_lg, self.b_const], writes=[b_tab])
                S.op("dve", lambda e: e.tensor_scalar(out=KD[:, 0:2], in0=KD[:, 0:2], scalar1=RET_KSCALE, scalar2=None, op0=ALU.mult),
                     reads=[b_tab], writes=[b_tab])
                for ci in range(3):
                    w, bw = wch[nh % 2], b_wch[nh % 2]
                    nh += 1
                    cc = hh * 1536 + ci * 512
                    S.dma("pool", lambda e, w=w, cc=cc: e.dma_start(out=w[:], in_=wv[:, :, cc:cc + 512]), writes=[bw])
                    for g in range(18):
                        ht, bh = hts[nin % 3], b_hts[nin % 3]
                        p, bp = pin[nin % 2], b_pin[nin % 2]
                        nin += 1
                        pos = g2p(g)
                        S.dma("sp", lambda e, ht=ht, pos=pos: e.dma_start(out=ht[:], in_=hT_d[pos]), reads=[self.b_hT], writes=[bh])
                        for kt in range(NKT):
                            S.op("pe", lambda e, p=p, ht=ht, w=w, kt=kt: e.matmul(
                                p[:], lhsT=ht[:, kt, :], rhs=w[:, kt, :], start=(kt == 0), stop=(kt == NKT - 1)),
                                reads=[bh, bw], writes=[bp])
                        if ci == 0:
                            if g < 2:
                                S.op("act", lambda e, p=p, g=g: e.copy(out=QK[:, g, :], in_=p[:]), reads=[bp], writes=[b_QK[g]])
                            else:
                                r, br = raw[g % 2], b_raw[g % 2]
                                S.op("act", lambda e, p=p, r=r: e.copy(out=r[:], in_=p[:]), reads=[bp], writes=[br])
                                X = r[:].rearrange("p (a b c d) -> p a b c d", a=2, b=2, c=2)
                                O = QK[:, g, :].rearrange("p (a b c d) -> p a b c d", a=2, b=2, c=2)
                                A_, B_ = X[:, :, :, 0, :], X[:, :, :, 1, :]
                                cosb = COS[:, g - 2].unsqueeze(1).to_broadcast([128, 2, 2, 64])
                                sinb = SIN[:, g - 2].unsqueeze(1).to_broadcast([128, 2, 2, 64])
                                S.op("dve", lambda e, A_=A_, cosb=cosb: e.tensor_tensor(out=rt[0][:], in0=A_, in1=cosb, op=ALU.mult),
                                     reads=[br, b_rope], writes=[b_rt[0]])
                                S.op("pool", lambda e, B_=B_, sinb=sinb: e.tensor_tensor(out=rt[1][:], in0=B_, in1=sinb, op=ALU.mult),
                                     reads=[br, b_rope], writes=[b_rt[1]])
                                S.op("dve", lambda e, O=O: e.tensor_tensor(out=O[:, :, :, 0, :], in0=rt[0][:], in1=rt[1][:], op=ALU.subtract),
                                     reads=[b_rt[0], b_rt[1]], writes=[b_QK[g]])
                                S.op("pool", lambda e, A_=A_, sinb=sinb: e.tensor_tensor(out=rt[2][:], in0=A_, in1=sinb, op=ALU.mult),
                                     reads=[br, b_rope], writes=[b_rt[2]])
                                S.op("dve", lambda e, B_=B_, cosb=cosb: e.tensor_tensor(out=rt[3][:], in0=B_, in1=cosb, op=ALU.mult),
                                     reads=[br, b_rope], writes=[b_rt[3]])
                                S.op("pool", lambda e, O=O: e.tensor_tensor(out=O[:, :, :, 1, :], in0=rt[2][:], in1=rt[3][:], op=ALU.add),
                                     reads=[b_rt[2], b_rt[3]], writes=[b_QK[g]])
                        elif ci == 1:
                            S.op("act", lambda e, p=p, g=g: e.copy(out=V[:, g, :], in_=p[:]), reads=[bp], writes=[b_V[g]])
                        else:
                            S.op("act", lambda e, p=p, g=g: e.activation(out=G[:, g, :], in_=p[:], func=AF.Silu),
                                 reads=[bp], writes=[b_G[g]])
                if RET_STOP <= 1:
                    continue
                Sb, bSb = Sst[1], b_Sst[1]
                S.op("pool", lambda e: e.memset(Sb[:], 0.0), writes=[bSb])
                for n, c in enumerate(order_b):
                    S.op("act", lambda e, c=c: e.copy(out=SBS[:, c], in_=Sb[:]), reads=[bSb], writes=[b_SBS[c]])
                    if n == len(order_b) - 1:
                        break
                    ks, bks = ksc[n % 2], b_ksc[n % 2]
                    S.op("pool", lambda e, ks=ks, c=c: e.tensor_scalar(out=ks[:], in0=QK[:, c, 256:512], scalar1=KD[:, 1:2], scalar2=None,
                                                                        op0=ALU.mult), reads=[b_QK[c], b_tab], writes=[bks])
                    for dh in range(2):
                        S.op("pe", lambda e, ks=ks, c=c, dh=dh: e.matmul(pS[dh][:], lhsT=ks[:, dh * 128:(dh + 1) * 128], rhs=V[:, c, :],
                                                                          start=True, stop=True), reads=[bks, b_V[c]], writes=[b_pS[dh]])
                        S.op("dve", lambda e, dh=dh: e.scalar_tensor_tensor(out=Sb[:, dh, :], in0=Sb[:, dh, :], scalar=KD[:, 3:4], in1=pS[dh][:],
                                                                             op0=ALU.mult, op1=ALU.add), reads=[bSb, b_pS[dh], b_tab], writes=[bSb])
                if RET_STOP <= 2:
                    continue
                Sf, bSf = Sst[0], b_Sst[0]
                S.op("pool", lambda e: e.memset(Sf[:], 0.0), writes=[bSf])
                for c in range(18):
                    i2 = c % 2
                    S.op("act", lambda e, i2=i2: e.copy(out=sfb[i2][:], in_=Sf[:]), reads=[bSf], writes=[b_sfb[i2]])
                    for t4 in range(4):
                        S.op("pe", lambda e, c=c, t4=t4: e.transpose(out=ptr[:, t4, :], in_=QK[:, c, t4 * 128:(t4 + 1) * 128], identity=self.ident[:]),
                             reads=[b_QK[c], self.b_ident], writes=[b_ptrq])
                    S.op("dve", lambda e, i2=i2: e.tensor_copy(out=qkT[i2][:], in_=ptr[:, 0:4, :]), reads=[b_ptrq], writes=[b_qkT[i2]])
                    S.op("dve", lambda e, i2=i2: e.tensor_tensor(out=qfT[i2][:], in0=qkT[i2][:, 0:2, :],
                                                                 in1=QDF[:].unsqueeze(1).to_broadcast([128, 2, 128]), op=ALU.mult),
                         reads=[b_qkT[i2], b_tab], writes=[b_qfT[i2]])
                    S.op("pool", lambda e, i2=i2: e.tensor_tensor(out=qbT[i2][:], in0=qkT[i2][:, 0:2, :],
                                                                  in1=QDB[:].unsqueeze(1).to_broadcast([128, 2, 128]), op=ALU.mult),
                         reads=[b_qkT[i2], b_tab], writes=[b_qbT[i2]])
                    for dh in range(2):
                        S.op("pe", lambda e, i2=i2, dh=dh: e.matmul(psc, lhsT=qkT[i2][:, 2 + dh, :], rhs=qkT[i2][:, dh, :],
                                                                    start=(dh == 0), stop=(dh == 1)), reads=[b_qkT[i2]], writes=[b_psc])
                    S.op("dve", lambda e, i2=i2: e.tensor_tensor(out=Pm[i2][:], in0=psc, in1=Mt[:], op=ALU.mult),
                         reads=[b_psc, b_tab], writes=[b_Pm[i2]])
                    o, bo = po[i2], b_po[i2]
                    S.op("pe", lambda e, i2=i2, o=o, c=c: e.matmul(o[:], lhsT=Pm[i2][:], rhs=V[:, c, :], start=True, stop=False),
                         reads=[b_Pm[i2], b_V[c]], writes=[bo])
                    for dh in range(2):
                        S.op("pe", lambda e, i2=i2, o=o, dh=dh: e.matmul(o[:], lhsT=qfT[i2][:, dh, :], rhs=sfb[i2][:, dh, :], start=False, stop=False),
                             reads=[b_qfT[i2], b_sfb[i2]], writes=[bo])
                    for dh in range(2):
                        S.op("pe", lambda e, i2=i2, o=o, dh=dh, c=c: e.matmul(o[:], lhsT=qbT[i2][:, dh, :], rhs=SBS[:, c, dh, :], start=False, stop=(dh == 1)),
                             reads=[b_qbT[i2], b_SBS[c]], writes=[bo])
                    S.op("act", lambda e, i2=i2, o=o: e.activation(out=junk[:], in_=o[:], func=AF.Square, scale=512.0 ** -0.5, accum_out=hss[i2][:]),
                         reads=[bo], writes=[b_junk, b_hss[i2]])
                    S.op("dve", lambda e, i2=i2: e.tensor_scalar(out=hss[i2][:], in0=hss[i2][:], scalar1=EPS, scalar2=None, op0=ALU.add),
                         reads=[b_hss[i2]], writes=[b_hss[i2]])
                    S.op("act", lambda e, i2=i2: e.sqrt(out=hss[i2][:], in_=hss[i2][:]), reads=[b_hss[i2]], writes=[b_hss[i2]])
                    S.op("dve", lambda e, i2=i2: e.reciprocal(out=hss[i2][:], in_=hss[i2][:]), reads=[b_hss[i2]], writes=[b_hss[i2]])
                    S.op("dve", lambda e, i2=i2, o=o, c=c: e.scalar_tensor_tensor(out=Atok[i2][:], in0=o[:], scalar=hss[i2][:, 0:1], in1=G[:, c, :],
                                                                                   op0=ALU.mult, op1=ALU.mult),
                         reads=[bo, b_hss[i2], b_G[c]], writes=[b_Atok[i2]])
                    for e4 in range(4 if RET_STOP != 25 else 0):
                        S.op("pe", lambda e, i2=i2, e4=e4: e.transpose(out=ptra[:, e4, :], in_=Atok[i2][:, e4 * 128:(e4 + 1) * 128], identity=self.ident[:]),
                             reads=[b_Atok[i2], self.b_ident], writes=[b_ptra])
                    if RET_STOP != 25:
                        S.op("act", lambda e, i2=i2: e.copy(out=ATst[i2][:], in_=ptra[:, 0:4, :]), reads=[b_ptra], writes=[b_ATst[i2]])
                    pos = g2p(c)
                    r_, tl = pos // 9, pos % 9
                    if RET_STOP == 27:
                        for e4 in range(4):
                            S.dma("sp", lambda e, i2=i2, r_=r_, tl=tl, hh=hh, e4=e4: e.dma_start(
                                out=AT_d[r_, hh * 4 + e4, :, tl * 128:(tl + 1) * 128], in_=ATst[i2][:, e4, :]),
                                reads=[b_ATst[i2]], writes=[self.b_AT])
                    if RET_STOP not in (25, 26, 27):
                      S.dma("sp", lambda e, i2=i2, r_=r_, tl=tl, hh=hh: e.dma_start(
                        out=AT_d[r_, hh * 4:(hh + 1) * 4, :, tl * 128:(tl + 1) * 128].rearrange("e p i -> p e i"), in_=ATst[i2][:]),
                        reads=[b_ATst[i2]], writes=[self.b_AT])
                    if c < 17:
                        ks, bks = ksc[c % 2], b_ksc[c % 2]
                        S.op("pool", lambda e, ks=ks, c=c: e.tensor_scalar(out=ks[:], in0=QK[:, c, 256:512], scalar1=KD[:, 0:1], scalar2=None,
                                                                            op0=ALU.mult), reads=[b_QK[c], b_tab], writes=[bks])
                        for dh in range(2):
                            S.op("pe", lambda e, ks=ks, c=c, dh=dh: e.matmul(pS[dh][:], lhsT=ks[:, dh * 128:(dh + 1) * 128], rhs=V[:, c, :],
                                                                              start=True, stop=True), reads=[bks, b_V[c]], writes=[b_pS[dh]])
                            S.op("dve", lambda e, dh=dh: e.scalar_tensor_tensor(out=Sf[:, dh, :], in0=Sf[:, dh, :], scalar=KD[:, 2:3], in1=pS[dh][:],
                                                                                 op0=ALU.mult, op1=ALU.add), reads=[bSf, b_pS[dh], b_tab], writes=[bSf])
            S.emit()

    def phase_na(self, hT_d, win_d, utab_d, AT_d):
        S = self.S
        wv = win_d.rearrange("(kt p) n -> p kt n", p=128)
        groups = [(0, 256, "ctx")] + [(256 + 512 * i, 512, 1 + 4 * i if i < 2 else 10 + 4 * (i - 2)) for i in range(4)]
        with contextlib.ExitStack() as ph:
            HT = self.sb(ph, "HT", [128, 18, NKT, 128], BF16)
            b_HT = S.bufs(18, "HT")
            wch = [self.sb(ph, f"nw{i}", [128, NKT, 512], BF16) for i in range(2)]
            b_wch = S.bufs(2, "nw")
            U = [self.sb(ph, f"U{i}", [128, 14, 64], F32) for i in range(2)]
            b_U = S.bufs(2, "U")
            QT = self.sb(ph, "QT", [128, 2304], BF16); b_QT = S.buf("QT")
            KT = self.sb(ph, "KT", [128, 2304], BF16); b_KT = S.buf("KT")
            GT = self.sb(ph, "GTn", [128, 2304], BF16); b_GT = S.buf("GT")
            VT = self.sb(ph, "VT", [128, 2304], BF16); b_VT = S.buf("VT")
            VTOK = self.sb(ph, "VTOK", [128, 33, 128], BF16); b_VTOK = S.buf("VTOK")
            ones = self.sb(ph, "ones", [128, 128], BF16); b_ones = S.buf("ones")
            Ex = [self.sb(ph, f"Ex{i}", [128, 384], BF16) for i in range(2)]
            b_Ex = S.bufs(2, "Ex")
            tmp = [self.sb(ph, f"tmp{i}", [128, 256], F32) for i in range(2)]
            b_tmp = S.bufs(2, "tmp")
            rden = [self.sb(ph, f"rden{i}", [128, 128], F32) for i in range(2)]
            b_rden = S.bufs(2, "rden")
            t2 = [self.sb(ph, f"t2{i}", [128, 128], F32) for i in range(2)]
            b_t2 = S.bufs(2, "t2")
            ATst = [self.sb(ph, f"ATn{i}", [128, 2304], BF16) for i in range(2)]
            b_ATst = S.bufs(2, "ATn")
            pin = [self.ps(ph, f"npin{i}", [128, 512], F32) for i in range(2)]
            b_pin = S.bufs(2, "npin")
            pst = [self.ps(ph, f"pst{i}", [128, 512], F32) for i in range(2)]
            b_pst = S.bufs(2, "pst")
            pov = [self.ps(ph, f"pov{i}", [128, 512], F32) for i in range(2)]
            b_pov = S.bufs(2, "pov")
            pvt = [self.ps(ph, f"pvt{i}", [128, 8, 128], BF16) for i in range(2)]
            b_pvt = S.bufs(2, "pvt")

            S.op("pool", lambda e: e.memset(ones[:], 1.0), writes=[b_ones])
            for pos in range(18):
                S.dma("sp", lambda e, pos=pos: e.dma_start(out=HT[:, pos], in_=hT_d[pos]), reads=[self.b_hT], writes=[b_HT[pos]])
            nin = 0
            nun = 0
            nvt = 0
            for hh in range(16):
                w, bw = wch[hh % 2], b_wch[hh % 2]
                u, bu = U[hh % 2], b_U[hh % 2]
                at, bat = ATst[hh % 2], b_ATst[hh % 2]
                S.dma("pool", lambda e, w=w, hh=hh: e.dma_start(out=w[:], in_=wv[:, :, hh * 512:(hh + 1) * 512]), writes=[bw])
                S.dma("sp", lambda e, u=u, hh=hh: e.dma_start(out=u[:], in_=utab_d[hh]), writes=[bu])
                for cb in range(4):
                    for (tok0, ntok, p0) in groups:
                        p, bp = pin[nin % 2], b_pin[nin % 2]
                        nin += 1
                        if p0 == "ctx":
                            rds = [b_HT[0], b_HT[9]]
                        else:
                            rds = [b_HT[p0 + i] for i in range(4)]
                        for kt in range(NKT):
                            if p0 == "ctx":
                                rhs = HT[:, 0:18:9, kt, :]
                            else:
                                rhs = HT[:, p0:p0 + 4, kt, :]
                            S.op("pe", lambda e, p=p, w=w, cb=cb, kt=kt, rhs=rhs, ntok=ntok: e.matmul(
                                p[:, 0:ntok], lhsT=w[:, kt, cb * 128:(cb + 1) * 128], rhs=rhs, start=(kt == 0), stop=(kt == NKT - 1)),
                                reads=rds + [bw], writes=[bp])
                        if cb == 0:
                            S.op("act", lambda e, p=p, tok0=tok0, ntok=ntok: e.copy(out=QT[:, tok0:tok0 + ntok], in_=p[:, 0:ntok]),
                                 reads=[bp], writes=[b_QT])
                        elif cb == 1:
                            S.op("dve", lambda e, p=p, tok0=tok0, ntok=ntok: e.tensor_copy(out=KT[:, tok0:tok0 + ntok], in_=p[:, 0:ntok]),
                                 reads=[bp], writes=[b_KT])
                        elif cb == 2:
                            S.op("dve", lambda e, p=p, tok0=tok0, ntok=ntok: e.tensor_copy(out=VT[:, tok0:tok0 + ntok], in_=p[:, 0:ntok]),
                                 reads=[bp], writes=[b_VT])
                        else:
                            S.op("act", lambda e, p=p, tok0=tok0, ntok=ntok: e.activation(out=GT[:, tok0:tok0 + ntok], in_=p[:, 0:ntok], func=AF.Silu),
                                 reads=[bp], writes=[b_GT])
                offs = [128 * g for g in range(18)] + [256 + 64 + 128 * m for m in range(15)]
                for b0 in range(0, 33, 8):
                    nb = min(8, 33 - b0)
                    pv, bpv = pvt[nvt % 2], b_pvt[nvt % 2]
                    nvt += 1
                    for k in range(nb):
                        o_ = offs[b0 + k]
                        S.op("pe", lambda e, pv=pv, k=k, o_=o_: e.transpose(out=pv[:, k, :], in_=VT[:, o_:o_ + 128], identity=self.ident[:]),
                             reads=[b_VT, self.b_ident], writes=[bpv])
                    S.op("act", lambda e, pv=pv, b0=b0, nb=nb: e.copy(out=VTOK[:, b0:b0 + nb, :], in_=pv[:, 0:nb, :]),
                         reads=[bpv], writes=[b_VTOK])
                units = []
                for cg in range(2):
                    units.append(dict(nq=128, qtok=128 * cg, ktiles=[(0, 0), (128, 1)], nloc=0, dr0=0, pos=g2p(cg), poff=0))
                for a in range(32):
                    rs = min(max(a - 4, 0), 24)
                    kts = []
                    for t in range(4):
                        ktok = 256 + 64 * rs + 128 * t
                        vi = (2 + rs // 2 + t) if rs % 2 == 0 else (18 + (rs - 1) // 2 + t)
                        kts.append((ktok, vi))
                    kts += [(0, 0), (128, 1)]
                    units.append(dict(nq=64, qtok=256 + 64 * a, ktiles=kts, nloc=4, dr0=rs - a + 7, pos=g2p(2 + a // 2), poff=(a % 2) * 64))
                for un in units:
                    i2 = nun % 2
                    nun += 1
                    nq, qtok, kts, nloc, dr0 = un["nq"], un["qtok"], un["ktiles"], un["nloc"], un["dr0"]
                    st_, bst = pst[i2], b_pst[i2]
                    nk = len(kts)
                    for t, (ktok, vi) in enumerate(kts):
                        S.op("pe", lambda e, st_=st_, t=t, ktok=ktok, qtok=qtok, nq=nq: e.matmul(
                            st_[:, t * nq:(t + 1) * nq], lhsT=KT[:, ktok:ktok + 128], rhs=QT[:, qtok:qtok + nq], start=True, stop=True),
                            reads=[b_KT, b_QT], writes=[bst])
                    ex, bex = Ex[i2], b_Ex[i2]
                    if nloc:
                        tm, btm = tmp[i2], b_tmp[i2]
                        S.op("dve", lambda e, tm=tm, st_=st_, u=u, dr0=dr0: e.scalar_tensor_tensor(
                            out=tm[:].rearrange("p (t q) -> p t q", t=4), in0=st_[:, 0:256].rearrange("p (t q) -> p t q", t=4),
                            scalar=NA_SCALE, in1=u[:, dr0:dr0 + 7:2, :], op0=ALU.mult, op1=ALU.add),
                            reads=[bst, bu], writes=[btm])
                        S.op("act", lambda e, ex=ex, tm=tm: e.activation(out=ex[:, 0:256], in_=tm[:], func=AF.Exp),
                             reads=[btm], writes=[bex])
                        S.op("act", lambda e, ex=ex, st_=st_: e.activation(out=ex[:, 256:384], in_=st_[:, 256:384], func=AF.Exp, scale=NA_SCALE),
                             reads=[bst], writes=[bex])
                    else:
                        S.op("act", lambda e, ex=ex, st_=st_: e.activation(out=ex[:, 0:256], in_=st_[:, 0:256], func=AF.Exp, scale=NA_SCALE),
                             reads=[bst], writes=[bex])
                    ov, bov = pov[i2], b_pov[i2]
                    for t, (ktok, vi) in enumerate(kts):
                        S.op("pe", lambda e, ov=ov, t=t, vi=vi, ex=ex, nq=nq, nk=nk: e.matmul(
                            ov[:, 0:nq], lhsT=VTOK[:, vi, :], rhs=ex[:, t * nq:(t + 1) * nq], start=(t == 0), stop=(t == nk - 1)),
                            reads=[b_VTOK, bex], writes=[bov])
                    for t in range(nk):
                        S.op("pe", lambda e, ov=ov, t=t, ex=ex, nq=nq, nk=nk: e.matmul(
                            ov[:, 128:128 + nq], lhsT=ones[:], rhs=ex[:, t * nq:(t + 1) * nq], start=(t == 0), stop=(t == nk - 1)),
                            reads=[b_ones, bex], writes=[bov])
                    rd, brd = rden[i2], b_rden[i2]
                    tt, btt = t2[i2], b_t2[i2]
                    S.op("dve", lambda e, rd=rd, ov=ov, nq=nq: e.reciprocal(out=rd[:, 0:nq], in_=ov[:, 128:128 + nq]), reads=[bov], writes=[brd])
                    S.op("dve", lambda e, tt=tt, rd=rd, ov=ov, nq=nq: e.tensor_tensor(out=tt[:, 0:nq], in0=ov[:, 0:nq], in1=rd[:, 0:nq], op=ALU.mult),
                         reads=[bov, brd], writes=[btt])
                    ao = un["pos"] * 128 + un["poff"]
                    S.op("pool", lambda e, at=at, tt=tt, ao=ao, qtok=qtok, nq=nq: e.tensor_tensor(
                        out=at[:, ao:ao + nq], in0=tt[:, 0:nq], in1=GT[:, qtok:qtok + nq], op=ALU.mult),
                        reads=[btt, b_GT], writes=[bat])
                for r_ in range(2):
                    S.dma("sp", lambda e, at=at, r_=r_, hh=hh: e.dma_start(out=AT_d[r_, hh], in_=at[:, r_ * 1152:(r_ + 1) * 1152]),
                          reads=[bat], writes=[self.b_AT])
            S.emit()

    def phase_out(self, AT_d, wout_d, x_src, x_dst, gt_of_tile):
        S = self.S
        wv = wout_d.rearrange("(kt p) n -> p kt n", p=128)
        with contextlib.ExitStack() as ph:
            ATs = self.sb(ph, "ATs", [128, 32, 1152], BF16)
            b_ATs = S.bufs(4, "ATs")
            wch = [self.sb(ph, f"ow{i}", [128, 32, 512], BF16) for i in range(2)]
            b_wch = S.bufs(2, "ow")
            xq = [self.sb(ph, f"xq{i}", [128, 512], F32) for i in range(3)]
            b_xq = S.bufs(3, "xq")
            yq = [self.sb(ph, f"yq{i}", [128, 512], F32) for i in range(2)]
            b_yq = S.bufs(2, "yq")
            py = [self.ps(ph, f"py{i}", [128, 512], F32) for i in range(2)]
            b_py = S.bufs(2, "py")
            for q4 in range(4):
                src_, k0 = q4 // 2, (q4 % 2) * 8
                S.dma("sp", lambda e, q4=q4, src_=src_, k0=k0: e.dma_start(
                    out=ATs[:, q4 * 8:(q4 + 1) * 8, :], in_=AT_d[src_, k0:k0 + 8].rearrange("k p t -> p k t")),
                    reads=[self.b_AT], writes=[b_ATs[q4]])
            n = 0
            for cch in range(4):
                w, bw = wch[cch % 2], b_wch[cch % 2]
                for hf in range(2):
                    S.dma("pool", lambda e, w=w, cch=cch, hf=hf: e.dma_start(
                        out=w[:, hf * 16:(hf + 1) * 16, :], in_=wv[:, hf * 16:(hf + 1) * 16, cch * 512:(cch + 1) * 512]), writes=[bw])
                for t in range(9):
                    p, bp = py[n % 2], b_py[n % 2]
                    x_, bx = xq[n % 3], b_xq[n % 3]
                    y_, by = yq[n % 2], b_yq[n % 2]
                    n += 1
                    S.dma("sp", lambda e, x_=x_, t=t, cch=cch: e.dma_start(out=x_[:], in_=x_src(t)[:, cch * 512:(cch + 1) * 512]), writes=[bx])
                    for kt in range(32):
                        S.op("pe", lambda e, p=p, w=w, kt=kt, t=t: e.matmul(
                            p[:], lhsT=ATs[:, kt, t * 128:(t + 1) * 128], rhs=w[:, kt, :], start=(kt == 0), stop=(kt == 31)),
                            reads=[b_ATs[kt // 8], bw], writes=[bp])
                    gi = gt_of_tile(t)
                    S.op("dve", lambda e, y_=y_, p=p, gi=gi, cch=cch: e.tensor_tensor(
                        out=y_[:], in0=p[:], in1=self.M[gi][:, cch * 512:(cch + 1) * 512], op=ALU.mult),
                        reads=[bp, self.b_M[gi]], writes=[by])
                    S.op("pool", lambda e, y_=y_, x_=x_: e.tensor_tensor(out=x_[:], in0=x_[:], in1=y_[:], op=ALU.add),
                         reads=[by, bx], writes=[bx])
                    S.dma("sp", lambda e, x_=x_, t=t, cch=cch: e.dma_start(out=x_dst(t)[:, cch * 512:(cch + 1) * 512], in_=x_[:]),
                          reads=[bx], writes=[self.b_xd])
            S.emit()

    def phase_final(self, x_src, out_dst, fg_d, tiles):
        S = self.S
        with contextlib.ExitStack() as ph:
            fg = self.sb(ph, "fg", [128, D], F32)
            b_fg = S.buf()
            xt = [self.sb(ph, f"fx{i}", [128, D], F32) for i in range(2)]
            b_xt = S.bufs(2, "fx")
            junk = self.sb(ph, "fjunk", [128, D], BF16)
            b_junk = S.buf()
            ss = [self.sb(ph, f"fss{i}", [128, 1], F32) for i in range(2)]
            b_ss = S.bufs(2, "fss")
            yo = [self.sb(ph, f"fy{i}", [128, D], F32) for i in range(2)]
            b_yo = S.bufs(2, "fy")
            S.dma("sp", lambda e: e.dma_start(out=fg[:], in_=fg_d.partition_broadcast(128)), writes=[b_fg])
            for n, t in enumerate(tiles):
                i = n % 2
                S.dma("sp", lambda e, i=i, t=t: e.dma_start(out=xt[i][:], in_=x_src(t)), reads=[self.b_xd], writes=[b_xt[i]])
                S.op("act", lambda e, i=i: e.activation(out=junk[:], in_=xt[i][:], func=AF.Square, scale=float(D) ** -0.5, accum_out=ss[i][:]),
                     reads=[b_xt[i]], writes=[b_junk, b_ss[i]])
                S.op("dve", lambda e, i=i: e.tensor_scalar(out=ss[i][:], in0=ss[i][:], scalar1=EPS, scalar2=None, op0=ALU.add),
                     reads=[b_ss[i]], writes=[b_ss[i]])
                S.op("act", lambda e, i=i: e.sqrt(out=ss[i][:], in_=ss[i][:]), reads=[b_ss[i]], writes=[b_ss[i]])
                S.op("dve", lambda e, i=i: e.reciprocal(out=ss[i][:], in_=ss[i][:]), reads=[b_ss[i]], writes=[b_ss[i]])
                S.op("dve", lambda e, i=i: e.scalar_tensor_tensor(out=yo[i][:], in0=xt[i][:], scalar=ss[i][:, 0:1], in1=fg[:],
                                                                  op0=ALU.mult, op1=ALU.mult), reads=[b_xt[i], b_ss[i], b_fg], writes=[b_yo[i]])
                S.dma("sp", lambda e, i=i, n=n: e.dma_start(out=out_dst(n), in_=yo[i][:]), reads=[b_yo[i]], writes=[self.b_out])
            S.emit()


def build(mode):
    P = Prog(mode)
    nc, S = P.nc, P.S
    P.b_hT = S.buf("hT"); P.b_AT = S.buf("AT"); P.b_xd = S.buf("xd"); P.b_out = S.buf("out")
    cvec_d = P.din("cvec", [128, 2, NKT])
    const_d = P.din("const", [128, CW])
    modw_d = P.din("mod_w", [D, 3 * D])
    modb_d = P.din("mod_b", [1, 3 * D])
    if mode in ("A_ret", "A_na"):
        xfull_d = P.din("x_full", [18, 128, D])
        ng_d = P.din("norm_g", [1, D])
        AT_d = P.dout("AT", [2, 16, 128, 1152], BF16)
        hT_d = P.dscr("hT", [18, 128, NKT, 128], BF16)
        if mode == "A_ret":
            win_d = P.din("w_in", [D, 6144])
            dec_d = P.din("dec", [1, 8])
            cos_d = P.din("cos", [128, 16, 2, 64])
            sin_d = P.din("sin", [128, 16, 2, 64])
        else:
            win_d = P.din("w_in", [D, 8192])
            utab_d = P.din("utab", [16, 128, 14, 64])
        P.setup_persist(cvec_d, const_d)
        with contextlib.ExitStack() as mst:
            P.alloc_M(mst)
            P.phase_mod(modw_d, modb_d, 0, 2 * D, {(0, 0): 0, (0, 1): 1, (1, 0): 2, (1, 1): 3})
            P.phase_gs(ng_d, [1, 3])
            P.phase_norm(lambda pos: xfull_d[pos], list(range(18)), hT_d,
                         lambda pos: (2, 3) if pos in (0, 9) else (0, 1))
        if mode == "A_ret":
            P.phase_ret(hT_d, win_d, dec_d, cos_d, sin_d, AT_d)
        else:
            P.phase_na(hT_d, win_d, utab_d, AT_d)
    else:
        AT_d = P.din("ATr", [2, 16, 128, 1152], BF16)
        xown_d = P.din("x_own", [9, 128, D])
        wout_d = P.din("w_out", [E, D])
        xnew_d = P.dout("x_new", [9, 128, D])
        P.setup_persist(cvec_d, const_d)
        P.alloc_M(P.stack)
        P.phase_mod(modw_d, modb_d, 2 * D, D, {(0, 0): 0, (1, 0): 1})
        P.phase_out(AT_d, wout_d, lambda t: xown_d[t], lambda t: xnew_d[t], lambda t: 1 if t == 0 else 0)
        if mode == "B_last":
            fg_d = P.din("final_g", [1, D])
            out_d = P.dout("out", [8, 128, D])
            P.phase_final(lambda t: xnew_d[t], lambda n: out_d[n], fg_d, list(range(1, 9)))
    P.stack.close()
    return nc


def build_fused(nlayers=4, stop=99):
    P = Prog("fused")
    nc, S = P.nc, P.S
    P.b_hT = S.buf("hT"); P.b_AT = S.buf("AT"); P.b_xd = S.buf("xd"); P.b_out = S.buf("out")
    cvec_d = P.din("cvec", [128, 2, NKT])
    const_d = P.din("const", [128, CW])
    modw_d = [P.din(f"mod_w{l}", [D, 3 * D]) for l in range(4)]
    modb_d = P.din("mod_b", [4, 1, 3 * D])
    ng_d = P.din("norm_g", [4, 1, D])
    xfull_d = P.din("x_full", [18, 128, D])
    rwin_d = {(j, sh): P.din(f"ret_win{j}{sh}", [D, 6144]) for j in range(2) for sh in range(2)}
    dec_d = P.din("dec", [2, 2, 1, 8])
    cos_d = P.din("cos", [128, 16, 2, 64])
    sin_d = P.din("sin", [128, 16, 2, 64])
    nwin_d = {(j, sh): P.din(f"na_win{j}{sh}", [D, 8192]) for j in range(2) for sh in range(2)}
    utab_d = P.din("utab", [2, 2, 16, 128, 14, 64])
    rwout_d = P.din("ret_wout", [2, E, D])
    nwout_d = P.din("na_wout", [2, E, D])
    fg_d = P.din("final_g", [1, D])
    out_d = P.dout("out", [16, 128, D])
    xbuf = P.dscr("xbuf", [18, 128, D], F32)
    hT_d = P.dscr("hT", [18, 128, NKT, 128], BF16)
    ATf = P.dscr("ATf", [2, 2, 16, 128, 1152], BF16)
    P.setup_persist(cvec_d, const_d)
    for l in range(nlayers):
        j = l // 2
        xs = xfull_d if l == 0 else xbuf
        with contextlib.ExitStack() as mst:
            P.alloc_M(mst)
            P.phase_mod(modw_d[l], modb_d[l], 0, 2 * D, {(0, 0): 0, (0, 1): 1, (1, 0): 2, (1, 1): 3})
            P.phase_gs(ng_d[l], [1, 3])
            P.phase_norm(lambda pos, xs=xs: xs[pos], list(range(18)), hT_d,
                         lambda pos: (2, 3) if pos in (0, 9) else (0, 1))
        if stop <= 1:
            break
        for sh in ([1] if stop == 12 else range(2 if stop > 2 else 1)):
            if l % 2 == 0:
                P.phase_ret(hT_d, rwin_d[(j, sh)], dec_d[j, sh], cos_d, sin_d, ATf[sh])
            else:
                P.phase_na(hT_d, nwin_d[(j, sh)], utab_d[j, sh], ATf[sh])
        if stop <= 3:
            break
        with contextlib.ExitStack() as mst:
            P.alloc_M(mst)
            P.phase_mod(modw_d[l], modb_d[l], 2 * D, D, {(0, 0): 0, (1, 0): 1})
            wout = rwout_d[j] if l % 2 == 0 else nwout_d[j]
            for r in range(2):
                P.phase_out(ATf[:, r], wout, lambda t, xs=xs, r=r: xs[r * 9 + t], lambda t, r=r: xbuf[r * 9 + t],
                            lambda t: 1 if t == 0 else 0)
    P.phase_final(lambda c: xbuf[g2p(2 + c)], lambda n: out_d[n], fg_d, list(range(16)))
    P.stack.close()
    return nc


_PROGS = {}
_DBG = None
FUSED = False
NLAYERS = 4
NCORES = 8
STOP = 99
RET_STOP = 3
DGE_PRECOOK = True


def get_prog(mode):
    if mode not in _PROGS:
        _PROGS[mode] = build(mode)
    return _PROGS[mode]


def kernel(x, c, ctx, c_ctx, mod_w, mod_b, norm_g, ret_w_in, ret_decay_fwd, ret_decay_bwd, ret_w_out,
           na_w_in, na_rpb, na_w_out, final_g):
    f32 = np.float32
    x = np.asarray(x, f32); ctx = np.asarray(ctx, f32)
    const = make_const()
    cos, sin = make_rope()
    if FUSED:
        return kernel_fused(x, c, ctx, c_ctx, mod_w, mod_b, norm_g, ret_w_in, ret_decay_fwd, ret_decay_bwd, ret_w_out,
                            na_w_in, na_rpb, na_w_out, final_g, const, cos, sin)
    cores = [(b, s) for b in range(4) for s in range(2)]
    xo = []
    cvecs = []
    for (b, s) in cores:
        t = np.concatenate([ctx[b, 128 * s:128 * (s + 1)], x[b, 1024 * s:1024 * (s + 1)]], axis=0).reshape(9, 128, D)
        xo.append(np.ascontiguousarray(t))
        cv = np.stack([np.asarray(c[b], f32).reshape(NKT, 128).T, np.asarray(c_ctx, f32).reshape(NKT, 128).T], axis=1)
        cvecs.append(np.ascontiguousarray(cv))
    out = None
    for l in range(4):
        j = l // 2
        mw = np.ascontiguousarray(mod_w[l], dtype=f32)
        mb = np.ascontiguousarray(mod_b[l], dtype=f32).reshape(1, 3 * D)
        ng = np.ascontiguousarray(norm_g[l], dtype=f32).reshape(1, D)
        in_maps = []
        for ci, (b, s) in enumerate(cores):
            xfull = np.concatenate([xo[2 * b], xo[2 * b + 1]], axis=0)
            m = {"cvec": cvecs[ci], "const": const, "mod_w": mw, "mod_b": mb, "x_full": xfull, "norm_g": ng}
            if l % 2 == 0:
                W = ret_w_in[j]
                parts = []
                for h in range(4 * s, 4 * s + 4):
                    parts += [W[:, h * 256:(h + 1) * 256], W[:, 2048 + h * 256:2048 + (h + 1) * 256],
                              W[:, 4096 + h * 512:4096 + (h + 1) * 512], W[:, 8192 + h * 512:8192 + (h + 1) * 512]]
                m["w_in"] = np.ascontiguousarray(np.concatenate(parts, axis=1), dtype=f32)
                m["dec"] = np.concatenate([ret_decay_fwd[j][4 * s:4 * s + 4], ret_decay_bwd[j][4 * s:4 * s + 4]]).astype(f32).reshape(1, 8)
                m["cos"] = cos; m["sin"] = sin
            else:
                W = na_w_in[j]
                parts = []
                for h in range(16 * s, 16 * s + 16):
                    parts += [W[:, k * 4096 + h * 128:k * 4096 + (h + 1) * 128] for k in range(4)]
                m["w_in"] = np.ascontiguousarray(np.concatenate(parts, axis=1), dtype=f32)
                m["utab"] = make_utab(np.asarray(na_rpb[j][16 * s:16 * s + 16], f32))
            in_maps.append(m)
        nc = get_prog("A_ret" if l % 2 == 0 else "A_na")
        res = run_bass_kernel_spmd(nc, in_maps, core_ids=list(range(8)))
        ATs = [r["AT"] for r in res.results]
        del in_maps
        wout = np.ascontiguousarray((ret_w_out if l % 2 == 0 else na_w_out)[j], dtype=f32)
        in_maps = []
        for ci, (b, s) in enumerate(cores):
            atr = np.stack([ATs[2 * b][s], ATs[2 * b + 1][s]], axis=0)
            m = {"cvec": cvecs[ci], "const": const, "mod_w": mw, "mod_b": mb, "ATr": np.ascontiguousarray(atr),
                 "x_own": xo[ci], "w_out": wout}
            if l == 3:
                m["final_g"] = np.asarray(final_g, f32).reshape(1, D)
            in_maps.append(m)
        nc = get_prog("B_last" if l == 3 else "B")
        res = run_bass_kernel_spmd(nc, in_maps, core_ids=list(range(8)))
        xo = [np.asarray(r["x_new"]) for r in res.results]
        if _DBG is not None:
            _DBG(l, xo, ATs)
        if l == 3:
            out = np.empty((4, T, D), f32)
            for ci, (b, s) in enumerate(cores):
                out[b, 1024 * s:1024 * (s + 1)] = np.asarray(res.results[ci]["out"]).reshape(1024, D)
    return out


def kernel_fused(x, c, ctx, c_ctx, mod_w, mod_b, norm_g, ret_w_in, ret_decay_fwd, ret_decay_bwd, ret_w_out,
                 na_w_in, na_rpb, na_w_out, final_g, const, cos, sin):
    f32 = np.float32
    rwin = np.empty((2, 2, D, 6144), f32)
    nwin = np.empty((2, 2, D, 8192), f32)
    dec = np.empty((2, 2, 1, 8), f32)
    utab = np.empty((2, 2, 16, 128, 14, 64), f32)
    for j in range(2):
        W = np.asarray(ret_w_in[j], f32)
        for sh in range(2):
            for hh in range(4):
                h = 4 * sh + hh
                o = hh * 1536
                rwin[j, sh, :, o:o + 256] = W[:, h * 256:(h + 1) * 256]
                rwin[j, sh, :, o + 256:o + 512] = W[:, 2048 + h * 256:2048 + (h + 1) * 256]
                rwin[j, sh, :, o + 512:o + 1024] = W[:, 4096 + h * 512:4096 + (h + 1) * 512]
                rwin[j, sh, :, o + 1024:o + 1536] = W[:, 8192 + h * 512:8192 + (h + 1) * 512]
            dec[j, sh, 0, 0:4] = np.asarray(ret_decay_fwd[j], f32)[4 * sh:4 * sh + 4]
            dec[j, sh, 0, 4:8] = np.asarray(ret_decay_bwd[j], f32)[4 * sh:4 * sh + 4]
        W = np.asarray(na_w_in[j], f32)
        for sh in range(2):
            for hh in range(16):
                h = 16 * sh + hh
                for k in range(4):
                    nwin[j, sh, :, hh * 512 + k * 128:hh * 512 + (k + 1) * 128] = W[:, k * 4096 + h * 128:k * 4096 + (h + 1) * 128]
            utab[j, sh] = make_utab(np.asarray(na_rpb[j], f32)[16 * sh:16 * sh + 16])
    shared = {"const": const,
              "mod_b": np.ascontiguousarray(mod_b, dtype=f32).reshape(4, 1, 3 * D),
              "norm_g": np.ascontiguousarray(norm_g, dtype=f32).reshape(4, 1, D),
              "dec": dec, "cos": cos, "sin": sin, "utab": utab,
              "ret_wout": np.ascontiguousarray(ret_w_out, dtype=f32), "na_wout": np.ascontiguousarray(na_w_out, dtype=f32),
              "final_g": np.asarray(final_g, f32).reshape(1, D)}
    for l in range(4):
        shared[f"mod_w{l}"] = np.ascontiguousarray(mod_w[l], dtype=f32)
    for j in range(2):
        for sh in range(2):
            shared[f"ret_win{j}{sh}"] = rwin[j, sh]
            shared[f"na_win{j}{sh}"] = nwin[j, sh]
    in_maps = []
    for core in range(NCORES):
        b = core // 2
        tiles = []
        for s_ in range(2):
            tiles.append(ctx[b, 128 * s_:128 * (s_ + 1)])
            tiles.append(x[b, 1024 * s_:1024 * (s_ + 1)])
        xfull = np.ascontiguousarray(np.concatenate(tiles, axis=0).reshape(18, 128, D))
        cv = np.ascontiguousarray(np.stack([np.asarray(c[b], f32).reshape(NKT, 128).T,
                                            np.asarray(c_ctx, f32).reshape(NKT, 128).T], axis=1))
        m = dict(shared)
        m["x_full"] = xfull
        m["cvec"] = cv
        in_maps.append(m)
    if "fused" not in _PROGS:
        _PROGS["fused"] = build_fused(NLAYERS, STOP)
    res = run_bass_kernel_spmd(_PROGS["fused"], in_maps, core_ids=list(range(NCORES)))
    out = np.zeros((4, T, D), f32)
    for b in range(NCORES // 2):
        out[b] = np.asarray(res.results[2 * b]["out"]).reshape(T, D)
    return out
```
